# Optimizing a Trainium2 kernel written in Bass

```python
import jax
import jax.numpy as jnp
from jax import lax
import numpy as np

D_MODEL = 4096
BATCH = 1
SEQ = 8192
DEPTH = 1

D_MIX = D_MODEL
D_POOL = D_MIX // 2
POOL_WINDOWS = (2, 4, 8, 16)
N_POOL_GROUPS = len(POOL_WINDOWS)
POOL_GROUP = D_POOL // N_POOL_GROUPS
D_MLSTM = D_MIX - D_POOL
N_HEADS = 8
DV = D_MLSTM // N_HEADS
DQK = DV // 2
D_QK = 2 * N_HEADS * DQK
CONV_W = 4
CHUNK = 64
GATE_CAP = 15.0
D_FF = 4 * D_MODEL
D_PLE = 256
EPS = 1e-6
D_IN = D_POOL + D_QK + D_MLSTM + D_MLSTM + 2 * N_HEADS

kernel_name = 'hybrid_pool_mlstm_layer'


def rmsnorm(x, g):
    xf = x.astype(jnp.float32)
    y = xf * lax.rsqrt(jnp.mean(xf * xf, axis=-1, keepdims=True) + EPS)
    return (y * g.astype(jnp.float32)).astype(x.dtype)


def soft_cap(z):
    return GATE_CAP * jnp.tanh(z / GATE_CAP)


def causal_conv(x, w, b):
    S = x.shape[1]
    xp = jnp.pad(x, ((0, 0), (CONV_W - 1, 0), (0, 0)))
    y = b
    for j in range(CONV_W):
        y = y + w[j] * xp[:, j:j + S]
    return y


def pool_mixer(u, w_pool, scale):
    B, S, _ = u.shape
    uf = u.astype(jnp.float32)
    csp = jnp.concatenate([jnp.zeros((B, 1, D_POOL), jnp.float32), jnp.cumsum(uf, axis=1)], axis=1)
    t = jnp.arange(S)
    outs = []
    for gi, w in enumerate(POOL_WINDOWS):
        sl = slice(gi * POOL_GROUP, (gi + 1) * POOL_GROUP)
        c = csp[..., sl]
        lo = jnp.maximum(t + 1 - w, 0)
        win_sum = c[:, 1:] - c[:, lo]
        cnt = jnp.minimum(t + 1, w).astype(jnp.float32)[None, :, None]
        outs.append(win_sum / cnt - uf[..., sl])
    z = jnp.stack(outs, axis=2).astype(u.dtype)
    y = jnp.einsum('bsgc,gcd->bsgd', z, w_pool).reshape(B, S, D_POOL)
    return y * scale


def mlstm_chunkwise(q, k, v, li, lf):
    B, H, S, _ = q.shape
    NC = S // CHUNK
    q = q.reshape(B, H, NC, CHUNK, DQK)
    k = k.reshape(B, H, NC, CHUNK, DQK)
    v = v.reshape(B, H, NC, CHUNK, DV)
    li = li.reshape(B, H, NC, CHUNK)
    lf = lf.reshape(B, H, NC, CHUNK)
    b = jnp.cumsum(lf, axis=-1)
    g = b[..., -1]
    a = g[..., None] - b + li
    m_loc = jnp.max(a, axis=-1)
    wa = jnp.exp(a - m_loc[..., None])
    C_loc = jnp.einsum('bhcl,bhcld,bhcle->bhcde', wa, k, v)
    n_loc = jnp.einsum('bhcl,bhcld->bhcd', wa, k)

    def step(carry, xs):
        C, n, m = carry
        g_c, m_l, C_l, n_l = xs
        m_new = jnp.maximum(g_c + m, m_l)
        s_old = jnp.exp(g_c + m - m_new)
        s_loc = jnp.exp(m_l - m_new)
        C_new = s_old[..., None, None] * C + s_loc[..., None, None] * C_l
        n_new = s_old[..., None] * n + s_loc[..., None] * n_l
        return (C_new, n_new, m_new), (C, n, m)

    init = (jnp.zeros((B, H, DQK, DV), jnp.float32), jnp.zeros((B, H, DQK), jnp.float32),
            jnp.zeros((B, H), jnp.float32))
    xs = (jnp.moveaxis(g, 2, 0), jnp.moveaxis(m_loc, 2, 0), jnp.moveaxis(C_loc, 2, 0), jnp.moveaxis(n_loc, 2, 0))
    _, (C_prev, n_prev, m_prev) = lax.scan(step, init, xs)
    C_prev = jnp.moveaxis(C_prev, 0, 2)
    n_prev = jnp.moveaxis(n_prev, 0, 2)
    m_prev = jnp.moveaxis(m_prev, 0, 2)

    causal = jnp.tril(jnp.ones((CHUNK, CHUNK), dtype=bool))
    D = b[..., :, None] - b[..., None, :] + li[..., None, :]
    D = jnp.where(causal, D, -jnp.inf)
    e = b + m_prev[..., None]
    m_t = jnp.maximum(jnp.max(D, axis=-1), e)
    W = jnp.exp(D - m_t[..., None]) * jnp.einsum('bhctd,bhcsd->bhcts', q, k)
    w_inter = jnp.exp(e - m_t)
    num = jnp.einsum('bhcts,bhcse->bhcte', W, v) + w_inter[..., None] * jnp.einsum('bhctd,bhcde->bhcte', q, C_prev)
    den = jnp.sum(W, axis=-1) + w_inter * jnp.einsum('bhctd,bhcd->bhct', q, n_prev)
    h = num / jnp.maximum(jnp.abs(den), jnp.exp(-m_t))[..., None]
    return h.reshape(B, H, S, DV)


def mlstm_mixer(qk_pre, v, o_pre, gate_pre, conv_w, conv_b, b_i, b_f, g_head):
    B, S, _ = v.shape
    f32 = jnp.float32
    qk = jax.nn.silu(causal_conv(qk_pre, conv_w, conv_b)).astype(f32)
    qk = qk.reshape(B, S, 2, N_HEADS, DQK).transpose(2, 0, 3, 1, 4)
    q = qk[0]
    k = qk[1] * (DQK ** -0.5)
    vh = v.astype(f32).reshape(B, S, N_HEADS, DV).transpose(0, 2, 1, 3)
    gates = gate_pre.astype(f32)
    i_pre = soft_cap(gates[..., :N_HEADS] + b_i.astype(f32))
    f_pre = soft_cap(gates[..., N_HEADS:] + b_f.astype(f32))
    li = i_pre.transpose(0, 2, 1)
    lf = jax.nn.log_sigmoid(f_pre).transpose(0, 2, 1)
    h = mlstm_chunkwise(q, k, vh, li, lf)
    h = h * lax.rsqrt(jnp.mean(h * h, axis=-1, keepdims=True) + EPS)
    h = h * g_head.astype(f32).reshape(N_HEADS, 1, DV)
    h = h.transpose(0, 2, 1, 3).reshape(B, S, D_MLSTM)
    return (jax.nn.sigmoid(o_pre.astype(f32)) * h).astype(v.dtype)


def setup_inputs(seed: int = 0) -> dict:
    key = jax.random.key(seed)
    ks = jax.random.split(key, 20)
    f32 = jnp.float32
    nrm = lambda k, shape, s: jax.random.normal(k, shape, f32) * s
    return {
        'x': nrm(ks[0], (BATCH, SEQ, D_MODEL), 1.0),
        'p': nrm(ks[1], (DEPTH, BATCH, SEQ, D_PLE), 1.0),
        'g_mix': 1.0 + nrm(ks[2], (DEPTH, D_MODEL), 0.02),
        'w_in': nrm(ks[3], (DEPTH, D_MODEL, D_IN), D_MODEL ** -0.5),
        'conv_w': nrm(ks[4], (DEPTH, CONV_W, D_QK), CONV_W ** -0.5),
        'conv_b': nrm(ks[5], (DEPTH, D_QK), 0.02),
        'b_igate': nrm(ks[6], (DEPTH, N_HEADS), 0.1),
        'b_fgate': jnp.broadcast_to(jnp.linspace(3.0, 6.0, N_HEADS, dtype=f32), (DEPTH, N_HEADS)) + nrm(ks[7], (DEPTH, N_HEADS), 0.01),
        'g_head': 1.0 + nrm(ks[8], (DEPTH, D_MLSTM), 0.02),
        'w_pool': nrm(ks[9], (DEPTH, N_POOL_GROUPS, POOL_GROUP, POOL_GROUP), POOL_GROUP ** -0.5),
        'pool_scale': 1.0 + nrm(ks[10], (DEPTH, D_POOL), 0.1),
        'w_out': nrm(ks[11], (DEPTH, D_MIX, D_MODEL), D_MIX ** -0.5),
        'g_mlp': 1.0 + nrm(ks[12], (DEPTH, D_MODEL), 0.02),
        'w_up': nrm(ks[13], (DEPTH, D_MODEL, D_FF), D_MODEL ** -0.5),
        'w_down': nrm(ks[14], (DEPTH, D_FF, D_MODEL), D_FF ** -0.5),
        'g_ple': 1.0 + nrm(ks[15], (DEPTH, D_MODEL), 0.02),
        'w_ple_gate': nrm(ks[16], (DEPTH, D_MODEL, D_MODEL), D_MODEL ** -0.5),
        'b_ple_gate': nrm(ks[17], (DEPTH, D_MODEL), 0.02),
        'w_ple': nrm(ks[18], (DEPTH, D_PLE, D_MODEL), D_PLE ** -0.5),
        'g_final': 1.0 + nrm(ks[19], (D_MODEL,), 0.02),
    }


def reference(x, p, g_mix, w_in, conv_w, conv_b, b_igate, b_fgate, g_head, w_pool, pool_scale,
              w_out, g_mlp, w_up, w_down, g_ple, w_ple_gate, b_ple_gate, w_ple, g_final):
    o0 = D_POOL
    o1 = o0 + D_QK
    o2 = o1 + D_MLSTM
    o3 = o2 + D_MLSTM
    h = x
    for l in range(DEPTH):
        hn = rmsnorm(h, g_mix[l])
        proj = hn @ w_in[l]
        y_pool = pool_mixer(proj[..., :o0], w_pool[l], pool_scale[l])
        y_mlstm = mlstm_mixer(proj[..., o0:o1], proj[..., o1:o2], proj[..., o2:o3], proj[..., o3:],
                              conv_w[l], conv_b[l], b_igate[l], b_fgate[l], g_head[l])
        h = h + jnp.concatenate([y_pool, y_mlstm], axis=-1) @ w_out[l]
        hn = rmsnorm(h, g_mlp[l])
        h = h + jnp.square(jax.nn.relu(hn @ w_up[l])) @ w_down[l]
        hn = rmsnorm(h, g_ple[l])
        gate = jax.nn.sigmoid(hn @ w_ple_gate[l] + b_ple_gate[l])
        h = h + gate * (p[l] @ w_ple[l])
    return rmsnorm(h, g_final)
```

```python
import contextlib
import numpy as np
import ml_dtypes
import concourse.bass as bass
import concourse.mybir as mybir
from concourse.bass_utils import run_bass_kernel_spmd

F32 = mybir.dt.float32
BF16 = mybir.dt.bfloat16
AF = mybir.ActivationFunctionType
ALU = mybir.AluOpType

NCORES = 8
D = 4096
KC = D // 128
SEQ = 8192
T = SEQ // NCORES
TB = 512
NBLK = T // TB
NTT = TB // 128
HALO = 16
TBX = TB + HALO
H = 8
DQK = 128
DV = 256
DFF = 16384
DPLE = 256
D_IN = 8208
O_POOL, O_Q, O_K, O_V, O_O, O_G = 0, 2048, 3072, 4096, 6144, 8192
EPS = 1e-6
CAP = 15.0
CW = 8 * 257 + 8
RING = 4
BKC = 8
DEBUG = False


class Buf:
    __slots__ = ("w", "r")

    def __init__(self):
        self.w = None
        self.r = []


class Prog:
    ENGS = ("pe", "act", "dve", "pool", "sp")

    def __init__(self, nc):
        self.nc = nc
        self.ops = {e: [] for e in self.ENGS}
        self.cnt = {}
        self.waited = {e: {} for e in self.ENGS}
        self.sems = {}
        self._cm = []
        for e in self.ENGS:
            self.new_sem(("eng", e))

    def new_sem(self, key):
        cm = self.nc.semaphore("s%d" % len(self.sems))
        self.sems[key] = cm.__enter__()
        self._cm.append(cm)
        self.cnt[key] = 0
        return key

    def _deps(self, reads, writes):
        deps = []
        for b in reads:
            if b.w is not None:
                deps.append(b.w)
        for b in writes:
            if b.w is not None:
                deps.append(b.w)
            deps.extend(b.r)
        return deps

    def _emit_waits(self, eng, deps, skip_key=None):
        need = {}
        for (k, v) in deps:
            if k == skip_key:
                continue
            if self.waited[eng].get(k, 0) >= v:
                continue
            if need.get(k, 0) < v:
                need[k] = v
        for k, v in need.items():
            self.waited[eng][k] = v
            h = self.sems[k]
            self.ops[eng].append(lambda E, h=h, v=v: E.wait_ge(h, v))

    def _mark(self, ev, reads, writes):
        for b in reads:
            b.r.append(ev)
            if len(b.r) > 24:
                best = {}
                for (k, v) in b.r:
                    if best.get(k, 0) < v:
                        best[k] = v
                b.r = list(best.items())
        for b in writes:
            b.w = ev
            b.r = []

    def op(self, eng, fn, reads=(), writes=(), signal=True):
        deps = self._deps(reads, writes)
        key = ("eng", eng)
        self._emit_waits(eng, deps, skip_key=(key if eng == "pe" else None))
        if signal:
            self.cnt[key] += 1
            h = self.sems[key]
            self.ops[eng].append(lambda E, fn=fn, h=h: fn(E).then_inc(h, 1))
            ev = (key, self.cnt[key])
        else:
            self.ops[eng].append(lambda E, fn=fn: fn(E))
            ev = (key, self.cnt[key] + 1)
        self._mark(ev, reads, writes)
        return ev

    def dma(self, eng, semkey, fn, reads=(), writes=(), inc=16):
        deps = self._deps(reads, writes)
        self._emit_waits(eng, deps)
        self.cnt[semkey] += inc
        h = self.sems[semkey]
        self.ops[eng].append(lambda E, fn=fn, h=h, inc=inc: fn(E).then_inc(h, inc))
        ev = (semkey, self.cnt[semkey])
        self._mark(ev, reads, writes)
        return ev

    def barrier(self, engs=("pe", "act", "dve", "sp")):
        evs = [(k, v) for k, v in self.cnt.items()
               if v > 0 and not (isinstance(k, tuple) and k[0] == "ring") and k != ("eng", "pool") and k != "cc"]
        for e in engs:
            self._emit_waits(e, evs, skip_key=None)

    def wait_event(self, eng, ev):
        self._emit_waits(eng, [ev])

    def run(self):
        with self.nc.Block() as block:
            @block.tensor
            def _(E):
                for f in self.ops["pe"]:
                    f(E)

            @block.scalar
            def _(E):
                for f in self.ops["act"]:
                    f(E)

            @block.vector
            def _(E):
                for f in self.ops["dve"]:
                    f(E)

            @block.gpsimd
            def _(E):
                for f in self.ops["pool"]:
                    f(E)

            @block.sync
            def _(E):
                for f in self.ops["sp"]:
                    f(E)

    def close(self):
        for cm in reversed(self._cm):
            cm.__exit__(None, None, None)


def build_nc():
    nc = bass.Bass("TRN2", target_bir_lowering=False)
    es = contextlib.ExitStack()

    def din(name, shape, dt=F32):
        return nc.dram_tensor(name, list(shape), dt, kind="ExternalInput").ap()

    x_ext = din("x_ext", [T + HALO, D])
    p_in = din("p", [T, DPLE])
    w_in = din("w_in", [D, D_IN])
    w_pool = din("w_pool", [4 * 512, 512])
    w_out = din("w_out", [D, D])
    w_up = din("w_up", [D, DFF])
    w_down = din("w_down", [DFF, D])
    w_gate = din("w_gate", [D, D])
    w_ple = din("w_ple", [DPLE, D])
    gT_in = din("gT", [128, 3 * KC])
    cw_in = din("cw", [128, 16, 5])
    psc_in = din("psc", [128, 16])
    bif_in = din("bif", [128, 16])
    ghead_in = din("g_head", [2048])
    bgate_in = din("b_gate", [D])
    gfin_in = din("g_final", [D])
    onehot_in = din("onehot", [128, 8])
    icnt_in = din("icnt", [128, NBLK, 4, 16])
    cst_in = din("cst", [128, 3, 128])
    cstb_in = din("cstb", [128, 2, 128], BF16)
    out = nc.dram_tensor("out", [T, D], F32, kind="ExternalOutput").ap()
    kT_s = nc.dram_tensor("kT_s", [NBLK, 128, H * TB], BF16).ap()
    v_s = nc.dram_tensor("v_s", [NBLK, 128, NTT * 2048], BF16).ap()
    cc_in = nc.dram_tensor("cc_in", [128, CW], F32)
    cc_out = nc.dram_tensor("cc_out", [NCORES * 128, CW], F32)

    P = Prog(nc)
    dbg_names = []

    def dump(name, src, shape, dt=F32):
        if not DEBUG:
            return
        import os
        sel = os.environ.get("DBGSEL", "")
        if sel and not any(name.startswith(x) for x in sel.split(",")):
            return
        P.barrier()
        t = nc.dram_tensor("dbg_" + name, list(shape), dt, kind="ExternalOutput").ap()
        dbg_names.append("dbg_" + name)
        P.dma("sp", s_ld, lambda E: E.dma_start(out=t, in_=src))
        P.barrier()

    def sb(name, shape, dt):
        return es.enter_context(nc.sbuf_tensor("sb_" + name, list(shape), dt))

    cst = sb("cst", [128, 3, 128], F32)
    cstb = sb("cstb", [128, 2, 128], BF16)
    gT = sb("gT", [128, 3 * KC], F32)
    cw = sb("cw", [128, 16, 5], F32)
    psc = sb("psc", [128, 16], F32)
    bif = sb("bif", [128, 16], F32)
    onehot = sb("onehot", [128, 8], F32)
    icnt = sb("icnt", [128, NBLK, 4, 16], F32)
    onesb = sb("onesb", [128, 2], BF16)
    A_ = sb("A_", [128, 8, 8], F32)
    Abf = sb("Abf", [128, 8, 8], BF16)
    EB = sb("EB", [128, 8, 8], F32)
    EG = sb("EG", [128, 8, 8], F32)
    C = sb("C", [128, 8, 257], F32)
    Cbf = sb("Cbf", [128, 8, 257], BF16)
    small = sb("small", [128, 64], F32)
    gpre = sb("gpre", [128, NTT, 16], F32)
    gtmp = sb("gtmp", [128, 6, NTT * 16], F32)
    ptmp = sb("ptmp", [128, 2, 16], F32)
    hn = sb("hn", [128, D], BF16)
    accr = sb("accr", [128, 16384], F32)
    hnT = sb("hnT", [128, KC, TBX], BF16)
    big = sb("big", [128, KC, TB], BF16)
    ring = [sb("ring%d" % i, [128, BKC, 512], BF16) for i in range(RING)]
    miscu = sb("miscu", [128, 4480], F32)

    acc = [accr[:, tt * D:(tt + 1) * D] for tt in range(NTT)]
    accb = accr[:].bitcast(BF16)
    qT = accb[:, 0:4096].rearrange("p (h t) -> p h t", h=H)
    kTb = accb[:, 4096:8192].rearrange("p (h t) -> p h t", h=H)
    vb = accb[:, 8192:16384].rearrange("p (a e) -> p a e", a=NTT)
    GO = accb[:, 16384:24576].rearrange("p (a e) -> p a e", a=NTT)
    numsb = accr[:, 12288:12288 + 2056].rearrange("p (h e) -> p h e", h=8)
    ytok = accb[:, 2 * (12288 + 2056):2 * (12288 + 2056) + 2048]
    Rt = accr[:, 0:2056].rearrange("p (h e) -> p h e", h=8)
    Cj = [accr[:, 4096 + i * 2304:4096 + i * 2304 + CW] for i in range(2)]
    bigf = big[:].rearrange("p k t -> p (k t)").bitcast(F32)
    xst = [bigf[:, i * D:(i + 1) * D] for i in range(2)]
    gfin = hnT[:].rearrange("p k t -> p (k t)").bitcast(F32)[:, 0:D]
    mub = miscu[:].bitcast(BF16)
    Et = [miscu[:, i * 528:(i + 1) * 528] for i in range(2)]
    slev = [miscu[:, 1056 + i * 528:1056 + (i + 1) * 528] for i in range(2)]
    av = [mub[:, i * 2048:(i + 1) * 2048].rearrange("p (h e) -> p h e", h=8) for i in range(2)]
    ktok = [mub[:, 4096 + i * 128:4096 + (i + 1) * 128] for i in range(2)]
    WT = [mub[:, 4352 + i * 128:4352 + (i + 1) * 128] for i in range(2)]
    cuet = [miscu[:, 2304 + i * 264:2304 + i * 264 + 257] for i in range(2)]
    sqj = miscu[:, 2832:3088]
    rtmp = [miscu[:, i * 512:(i + 1) * 512] for i in range(2)]
    gt1 = [miscu[:, 1024 + i * 512:1024 + (i + 1) * 512] for i in range(2)]
    gt2 = [miscu[:, 2048 + i * 512:2048 + (i + 1) * 512] for i in range(2)]
    bc512 = [miscu[:, 3072 + i * 512:3072 + (i + 1) * 512] for i in range(2)]
    pst = miscu[:, 4096:4352]
    pbf = mub[:, 8704:8960]
    pT = sb("pT", [128, 2, TB], BF16)

    ps = [es.enter_context(nc.psum_tensor("ps%d" % i, [128, 512], F32)) for i in range(8)]
    psb = [t[:].bitcast(BF16) for t in ps]

    ident = cstb[:, 0, :]
    mask01 = cstb[:, 1, :]
    Umat = cst[:, 0, :]
    sel127 = cst[:, 1, :]

    Bps = [Buf() for _ in range(8)]
    Bring = [Buf() for _ in range(RING)]
    Bconst = Buf()
    Bhn = Buf()
    BhnT = [Buf() for _ in range(NTT + 1)]
    Bacc = [Buf() for _ in range(NTT)]
    Bxst = [Buf(), Buf()]
    Bsmall = Buf()
    Bgates = Buf()
    BC = [Buf() for _ in range(H)]
    BCbf = [Buf() for _ in range(H)]
    Bq = [Buf() for _ in range(H)]
    Bk = [Buf() for _ in range(H)]
    Bv = [Buf() for _ in range(NTT)]
    BGO = [Buf() for _ in range(NTT)]
    Bbig = [Buf() for _ in range(KC)]
    BE = [Buf(), Buf()]
    Bsl = [Buf(), Buf()]
    Bav = [Buf(), Buf()]
    Bktok = [Buf(), Buf()]
    BWT = [Buf(), Buf()]
    Bnum = Buf()
    Bytok = Buf()
    Brt = [Buf(), Buf()]
    Bg1 = [Buf(), Buf()]
    Bg2 = [Buf(), Buf()]
    Bbc = [Buf(), Buf()]
    Bp = Buf()
    BpT = Buf()
    BR = Buf()
    BCj = [Buf(), Buf()]
    Bgfin = Buf()
    Bscr_k = [Buf() for _ in range(NBLK)]
    Bscr_v = [Buf() for _ in range(NBLK)]
    Bcc = Buf()

    s_ring = [P.new_sem(("ring", i)) for i in range(RING)]
    s_const = P.new_sem("const")
    s_x = [P.new_sem(("x", i)) for i in range(2)]
    s_acc = [P.new_sem(("acc", i)) for i in range(NTT)]
    s_misc = [P.new_sem(("misc", i)) for i in range(2)]
    s_bc = [P.new_sem(("bc", i)) for i in range(2)]
    s_p = P.new_sem("p")
    s_scr = P.new_sem("scr")
    s_ld = P.new_sem("ld")
    s_cc = P.new_sem("cc")
    s_cj = [P.new_sem(("cj", i)) for i in range(2)]
    s_out = P.new_sem("out")

    for (dst, src) in ((cst, cst_in), (cstb, cstb_in), (gT, gT_in), (cw, cw_in), (psc, psc_in),
                       (bif, bif_in), (onehot, onehot_in), (icnt, icnt_in)):
        P.dma("sp", s_const, lambda E, dst=dst, src=src: E.dma_start(out=dst[:], in_=src), writes=[Bconst])
    P.op("dve", lambda E: E.memset(onesb[:], 1.0), writes=[Bconst])

    wstate = {"i": 0}

    def load_block(W2d, k0, nkc, c0, ncols):
        i = wstate["i"]
        wstate["i"] += 1
        slot = i % RING
        src = W2d[k0:k0 + nkc * 128, c0:c0 + ncols].rearrange("(kc p) n -> p kc n", p=128)
        P.dma("pool", s_ring[slot],
              lambda E, slot=slot, src=src, nkc=nkc, ncols=ncols: E.dma_start(out=ring[slot][:, 0:nkc, 0:ncols], in_=src),
              writes=[Bring[slot]])
        return slot

    setctr = {"i": 0}

    def gemm_tok(W2d, K, c0, ncols, lhs_fn, lhs_bufs, ntt, epilogue, extra=None, width=512):
        nkb = (K + BKC * 128 - 1) // (BKC * 128)
        for fg in range((ncols + width - 1) // width):
            w = min(width, ncols - fg * width)
            st = setctr["i"] % 2
            setctr["i"] += 1
            for kb in range(nkb):
                nkc = min(BKC, K // 128 - kb * BKC)
                slot = load_block(W2d, kb * BKC * 128, nkc, c0 + fg * width, w)
                for tt in range(ntt):
                    b = st * 4 + tt
                    for kc in range(nkc):
                        first = (kb == 0 and kc == 0)
                        last = (kb == nkb - 1 and kc == nkc - 1) and extra is None
                        sig = last or (tt == ntt - 1 and kc == nkc - 1)
                        P.op("pe", lambda E, b=b, tt=tt, kk=kb * BKC + kc, kc=kc, slot=slot, w=w, first=first, last=last:
                             E.matmul(out=ps[b][:, 0:w], lhsT=lhs_fn(kk, tt), rhs=ring[slot][:, kc, 0:w], start=first, stop=last),
                             reads=[Bring[slot]] + lhs_bufs(tt), writes=[Bps[b]], signal=sig)
            if extra is not None:
                extra(fg, st)
            for tt in range(ntt):
                epilogue(fg, tt, st * 4 + tt, w)

    def gemm_feat(W2d, K, c0, ncols, rhs_fn, rhs_bufs, epilogue, halo):
        nkb = K // (BKC * 128)
        for fg in range(ncols // 512):
            if halo:
                st = 0
            else:
                st = setctr["i"] % 2
                setctr["i"] += 1
            for kb in range(nkb):
                slot = load_block(W2d, kb * BKC * 128, BKC, c0 + fg * 512, 512)
                for fc in range(4):
                    b = st * 4 + fc
                    for kc in range(BKC):
                        first = (kb == 0 and kc == 0)
                        last = (kb == nkb - 1 and kc == BKC - 1)
                        kk = kb * BKC + kc
                        P.op("pe", lambda E, b=b, fc=fc, kk=kk, kc=kc, slot=slot, first=first, last=last:
                             E.matmul(out=ps[b][:, :], lhsT=ring[slot][:, kc, fc * 128:(fc + 1) * 128], rhs=rhs_fn(kk, False),
                                      start=first, stop=last),
                             reads=[Bring[slot]] + rhs_bufs, writes=[Bps[b]], signal=((last or (fc == 3 and kc == BKC - 1)) and not halo))
                        if halo:
                            P.op("pe", lambda E, fc=fc, kk=kk, kc=kc, slot=slot, first=first, last=last:
                                 E.matmul(out=ps[4 + fc][:, 0:HALO], lhsT=ring[slot][:, kc, fc * 128:(fc + 1) * 128],
                                          rhs=rhs_fn(kk, True), start=first, stop=last),
                                 reads=[Bring[slot]] + rhs_bufs, writes=[Bps[4 + fc]], signal=(last or (fc == 3 and kc == BKC - 1)))
            for fc in range(4):
                epilogue(fg, fc, st * 4 + fc)

    def rstd_from_ss(col, npart, scale):
        c = small[0:npart, col:col + 1]
        P.op("dve", lambda E: E.tensor_scalar(out=c, in0=c, scalar1=scale, scalar2=EPS, op0=ALU.mult, op1=ALU.add),
             reads=[Bsmall], writes=[Bsmall])
        P.op("act", lambda E: E.activation(out=c, in_=c, func=AF.Sqrt), reads=[Bsmall], writes=[Bsmall])
        P.op("dve", lambda E: E.reciprocal(out=c, in_=c), reads=[Bsmall], writes=[Bsmall])

    def norm_T(src, Bsrc, npart, gsel, tcol0, Bdst):
        P.op("dve", lambda E: E.memset(small[0:npart, 0:1], 0.0), writes=[Bsmall])
        P.op("act", lambda E: E.activation(out=hn[0:npart, :], in_=src, func=AF.Square, accum_out=small[0:npart, 0:1]),
             reads=[Bsrc], writes=[Bhn, Bsmall])
        rstd_from_ss(0, npart, 1.0 / D)
        P.op("dve", lambda E: E.tensor_scalar_mul(out=hn[0:npart, :], in0=src, scalar1=small[0:npart, 0:1]),
             reads=[Bsrc, Bsmall], writes=[Bhn])
        for g4 in range(KC // 4):
            b = g4 % 4
            for j in range(4):
                kc = g4 * 4 + j
                P.op("pe", lambda E, b=b, j=j, kc=kc: E.transpose(out=psb[b][:, j * 128:j * 128 + npart],
                                                                  in_=hn[0:npart, kc * 128:(kc + 1) * 128],
                                                                  identity=ident[0:npart, 0:npart]),
                     reads=[Bhn, Bconst], writes=[Bps[b]], signal=(j == 3))
            eng = "dve" if g4 % 2 == 0 else "dve"
            P.op(eng, lambda E, b=b, g4=g4: E.tensor_tensor(
                out=hnT[:, g4 * 4:g4 * 4 + 4, tcol0:tcol0 + npart],
                in0=psb[b][:, 0:512].rearrange("p (j t) -> p j t", j=4)[:, :, 0:npart],
                in1=gT[:, gsel * KC + g4 * 4:gsel * KC + g4 * 4 + 4].unsqueeze(2).to_broadcast([128, 4, npart]),
                op=ALU.mult), reads=[Bps[b], Bconst], writes=[Bdst])

    def load_x_and_norm1(tb):
        r0 = tb * TB
        P.dma("sp", s_x[0], lambda E: E.dma_start(out=xst[0][0:HALO, :], in_=x_ext[r0:r0 + HALO, :]), writes=[Bxst[0]])
        norm_T(xst[0][0:HALO, :], Bxst[0], HALO, 0, 0, BhnT[NTT])
        for tt in range(NTT):
            i = (tt + 1) % 2
            rr = r0 + HALO + tt * 128
            P.dma("sp", s_x[i], lambda E, i=i, rr=rr: E.dma_start(out=xst[i][:, :], in_=x_ext[rr:rr + 128, :]), writes=[Bxst[i]])
            norm_T(xst[i][:, :], Bxst[i], 128, 0, HALO + tt * 128, BhnT[tt])

    def hnT_rhs(kk, halo):
        return hnT[:, kk, 0:HALO] if halo else hnT[:, kk, HALO:TBX]

    def hnT_lhs(kk, tt):
        return hnT[:, kk, HALO + tt * 128:HALO + (tt + 1) * 128]

    def conv_silu_epi(dstT, Bdst, ch0):
        def epi(fg, fc, b):
            ch = ch0 + fg * 4 + fc
            h = (fg * 4 + fc)
            e = (fg * 4 + fc) % 2
            P.op("act", lambda E: E.copy(out=Et[e][:, HALO:TBX], in_=ps[b][:, :]), reads=[Bps[b]], writes=[BE[e]])
            P.op("act", lambda E: E.copy(out=Et[e][:, 0:HALO], in_=ps[4 + fc][:, 0:HALO]),
                 reads=[Bps[4 + fc]], writes=[BE[e]])
            s = slev[e]
            P.op("dve", lambda E: E.tensor_scalar(out=s[:, 0:TB], in0=Et[e][:, HALO:TBX], scalar1=cw[:, ch, 3:4],
                                                  scalar2=cw[:, ch, 4:5], op0=ALU.mult, op1=ALU.add),
                 reads=[BE[e], Bconst], writes=[Bsl[e]])
            for j in range(3):
                P.op("dve", lambda E, j=j: E.scalar_tensor_tensor(out=s[:, 0:TB], in0=Et[e][:, HALO - 3 + j:HALO - 3 + j + TB],
                                                                 scalar=cw[:, ch, j:j + 1], in1=s[:, 0:TB],
                                                                 op0=ALU.mult, op1=ALU.add),
                     reads=[BE[e], Bconst, Bsl[e]], writes=[Bsl[e]])
            P.op("act", lambda E: E.activation(out=dstT[:, h, :], in_=s[:, 0:TB], func=AF.Silu), reads=[Bsl[e]], writes=[Bdst[h]])
        return epi

    def gates_block(tb):
        n = NTT * 16
        g2 = gpre[:].rearrange("p a c -> p (a c)")
        t0, t1, t2, t3 = (gtmp[:, i, :] for i in (0, 1, 2, 4))
        P.op("act", lambda E: E.activation(out=t0, in_=g2, func=AF.Tanh, scale=1.0 / CAP), reads=[Bgates], writes=[Bgates])
        P.op("dve", lambda E: E.tensor_scalar_mul(out=t0, in0=t0, scalar1=CAP), reads=[Bgates], writes=[Bgates])
        t0v = t0.rearrange("p (a c) -> p a c", a=NTT)
        li = t0v[:, :, 0:8]
        fp = t0v[:, :, 8:16]
        t1v = t1.rearrange("p (a c) -> p a c", a=NTT)[:, :, 0:8]
        t2v = t2.rearrange("p (a c) -> p a c", a=NTT)[:, :, 0:8]
        t3v = t3.rearrange("p (a c) -> p a c", a=NTT)[:, :, 0:8]
        P.op("act", lambda E: E.activation(out=t1v, in_=fp, func=AF.Exp, scale=-1.0), reads=[Bgates], writes=[Bgates])
        P.op("act", lambda E: E.activation(out=t1v, in_=t1v, func=AF.Ln, bias=1.0), reads=[Bgates], writes=[Bgates])
        lc = gtmp[:, 2, 0:NTT * 8].rearrange("p (a c) -> p a c", a=NTT)
        P.op("dve", lambda E: E.tensor_copy(out=lc, in_=t1v), reads=[Bgates], writes=[Bgates])
        lcf = gtmp[:, 2, 0:NTT * 8]
        P.op("pe", lambda E: E.matmul(out=ps[5][:, 0:NTT * 8], lhsT=Umat, rhs=lcf, start=True, stop=True),
             reads=[Bgates, Bconst], writes=[Bps[5]])
        cum = gtmp[:, 3, 0:NTT * 8]
        P.op("dve", lambda E: E.tensor_copy(out=cum, in_=ps[5][:, 0:NTT * 8]), reads=[Bps[5]], writes=[Bgates])
        cumv = cum.rearrange("p (a c) -> p a c", a=NTT)
        c0 = tb * NTT
        P.op("act", lambda E: E.activation(out=EB[:, c0:c0 + NTT, :], in_=cumv, func=AF.Exp, scale=-1.0), reads=[Bgates], writes=[Bgates])
        P.op("dve", lambda E: E.tensor_tensor(out=t3v, in0=li, in1=cumv, op=ALU.add), reads=[Bgates], writes=[Bgates])
        P.op("act", lambda E: E.activation(out=A_[:, c0:c0 + NTT, :], in_=t3v, func=AF.Exp), reads=[Bgates], writes=[Bgates])
        P.op("dve", lambda E: E.tensor_scalar_mul(out=A_[:, c0:c0 + NTT, :], in0=A_[:, c0:c0 + NTT, :], scalar1=float(DQK ** -0.5)), reads=[Bgates], writes=[Bgates])
        P.op("dve", lambda E: E.tensor_copy(out=Abf[:, c0:c0 + NTT, :], in_=A_[:, c0:c0 + NTT, :]), reads=[Bgates], writes=[Bgates])
        P.op("pe", lambda E: E.matmul(out=ps[6][:, 0:NTT * 8], lhsT=sel127, rhs=cum, start=True, stop=True),
             reads=[Bgates, Bconst], writes=[Bps[6]])
        P.op("act", lambda E: E.activation(out=EG[:, c0:c0 + NTT, :], in_=ps[6][:, 0:NTT * 8].rearrange("p (a c) -> p a c", a=NTT),
                                           func=AF.Exp, scale=-1.0), reads=[Bps[6]], writes=[Bgates])

    def state_update(c, tt, first_av):
        ai = c % 2
        P.op("dve", lambda E: E.tensor_tensor(out=av[ai][:, :, :], in0=vb[:, tt, :].rearrange("p (h e) -> p h e", h=8),
                                              in1=A_[:, c, :].unsqueeze(2).to_broadcast([128, 8, 256]), op=ALU.mult),
             reads=[Bv[tt], Bgates], writes=[Bav[ai]])
        for h in range(H):
            ki = h % 2
            pb = 6 + (h % 2)
            P.op("pe", lambda E, h=h, pb=pb: E.transpose(out=psb[pb][:, 768:896], in_=kTb[:, h, tt * 128:(tt + 1) * 128], identity=ident),
                 reads=[Bk[h], Bconst], writes=[Bps[pb]])
            P.op("act", lambda E, ki=ki, pb=pb: E.copy(out=ktok[ki], in_=psb[pb][:, 768:896]), reads=[Bps[pb]], writes=[Bktok[ki]])
            P.op("pe", lambda E, h=h, ki=ki, pb=pb: E.matmul(out=ps[pb][:, 0:256], lhsT=ktok[ki], rhs=av[ai][:, h, :], start=True, stop=True),
                 reads=[Bktok[ki], Bav[ai]], writes=[Bps[pb]], signal=False)
            P.op("pe", lambda E, h=h, ki=ki, pb=pb: E.matmul(out=ps[pb][:, 256:257], lhsT=ktok[ki], rhs=Abf[:, c, h:h + 1], start=True, stop=True),
                 reads=[Bktok[ki], Bgates], writes=[Bps[pb]])
            cue = cuet[h % 2]
            P.op("act", lambda E, h=h, pb=pb, cue=cue: E.activation(out=cue, in_=ps[pb][:, 0:257], func=AF.Copy, scale=EG[:, c, h:h + 1]),
                 reads=[Bps[pb], Bgates], writes=[Bg1[h % 2]])
            P.op("dve", lambda E, h=h, cue=cue: E.scalar_tensor_tensor(out=C[:, h, :], in0=C[:, h, :], scalar=EG[:, c, h:h + 1], in1=cue,
                                                                      op0=ALU.mult, op1=ALU.add),
                 reads=[BC[h], Bg1[h % 2], Bgates], writes=[BC[h]])
            P.op("dve", lambda E, h=h: E.tensor_copy(out=Cbf[:, h, :], in_=C[:, h, :]), reads=[BC[h]], writes=[BCbf[h]])

    def chunk_output(c, tt):
        P.op("dve", lambda E: E.memset(small[:, 8:16], 0.0), writes=[Bsmall])
        for h in range(H):
            wi = h % 2
            pb = 4 + (h % 2)
            P.op("pe", lambda E, h=h, wi=wi: E.matmul(out=ps[2 + wi][:, 0:128], lhsT=kTb[:, h, tt * 128:(tt + 1) * 128],
                                               rhs=qT[:, h, tt * 128:(tt + 1) * 128], start=True, stop=True),
                 reads=[Bk[h], Bq[h]], writes=[Bps[2 + wi]])
            P.op("dve", lambda E, h=h, wi=wi: E.scalar_tensor_tensor(out=WT[wi], in0=ps[2 + wi][:, 0:128], scalar=A_[:, c, h:h + 1],
                                                                    in1=mask01, op0=ALU.mult, op1=ALU.mult),
                 reads=[Bps[2 + wi], Bgates, Bconst], writes=[BWT[wi]])
            P.op("pe", lambda E, h=h, wi=wi, pb=pb: E.matmul(out=ps[pb][:, 0:256], lhsT=WT[wi], rhs=vb[:, tt, h * 256:(h + 1) * 256],
                                                             start=True, stop=False),
                 reads=[BWT[wi], Bv[tt]], writes=[Bps[pb]], signal=False)
            P.op("pe", lambda E, h=h, pb=pb: E.matmul(out=ps[pb][:, 0:256], lhsT=qT[:, h, tt * 128:(tt + 1) * 128], rhs=Cbf[:, h, 0:256],
                                                      start=False, stop=True),
                 reads=[Bq[h], BCbf[h]], writes=[Bps[pb]], signal=False)
            P.op("pe", lambda E, wi=wi, pb=pb: E.matmul(out=ps[pb][:, 256:257], lhsT=WT[wi], rhs=onesb[:, 0:1], start=True, stop=False),
                 reads=[BWT[wi], Bconst], writes=[Bps[pb]], signal=False)
            P.op("pe", lambda E, h=h, pb=pb: E.matmul(out=ps[pb][:, 256:257], lhsT=qT[:, h, tt * 128:(tt + 1) * 128], rhs=Cbf[:, h, 256:257],
                                                      start=False, stop=True),
                 reads=[Bq[h], BCbf[h]], writes=[Bps[pb]])
            P.op("act", lambda E, h=h, pb=pb: E.copy(out=numsb[:, h, :], in_=ps[pb][:, 0:257]), reads=[Bps[pb]], writes=[Bnum])
            P.op("act", lambda E, h=h: E.activation(out=sqj, in_=numsb[:, h, 0:256], func=AF.Square,
                                                    accum_out=small[:, 8 + h:9 + h]),
                 reads=[Bnum], writes=[Bg2[0], Bsmall])
        ssq = small[:, 8:16]
        den = small[:, 16:24]
        r = small[:, 24:32]
        t = small[:, 32:40]
        ebc = EB[:, c, :]
        P.op("dve", lambda E: E.tensor_tensor(out=den, in0=numsb[:, :, 256], in1=ebc, op=ALU.mult), reads=[Bnum, Bgates], writes=[Bsmall])
        P.op("dve", lambda E: E.tensor_scalar_mul(out=t, in0=den, scalar1=-1.0), reads=[Bsmall], writes=[Bsmall])
        P.op("dve", lambda E: E.tensor_tensor(out=den, in0=den, in1=t, op=ALU.max), reads=[Bsmall], writes=[Bsmall])
        P.op("dve", lambda E: E.tensor_scalar_max(out=den, in0=den, scalar1=1.0), reads=[Bsmall], writes=[Bsmall])
        P.op("dve", lambda E: E.reciprocal(out=den, in_=den), reads=[Bsmall], writes=[Bsmall])
        P.op("dve", lambda E: E.tensor_tensor(out=r, in0=ebc, in1=den, op=ALU.mult), reads=[Bsmall, Bgates], writes=[Bsmall])
        P.op("dve", lambda E: E.tensor_tensor(out=t, in0=ssq, in1=r, op=ALU.mult), reads=[Bsmall], writes=[Bsmall])
        P.op("dve", lambda E: E.tensor_tensor(out=t, in0=t, in1=r, op=ALU.mult), reads=[Bsmall], writes=[Bsmall])
        P.op("dve", lambda E: E.tensor_scalar(out=t, in0=t, scalar1=1.0 / DV, scalar2=EPS, op0=ALU.mult, op1=ALU.add), reads=[Bsmall], writes=[Bsmall])
        P.op("act", lambda E: E.activation(out=t, in_=t, func=AF.Sqrt), reads=[Bsmall], writes=[Bsmall])
        P.op("dve", lambda E: E.reciprocal(out=t, in_=t), reads=[Bsmall], writes=[Bsmall])
        P.op("dve", lambda E: E.tensor_tensor(out=t, in0=t, in1=r, op=ALU.mult), reads=[Bsmall], writes=[Bsmall])
        for h in range(H):
            P.op("dve", lambda E, h=h: E.scalar_tensor_tensor(out=ytok[:, h * 256:(h + 1) * 256], in0=numsb[:, h, 0:256], scalar=small[:, 32 + h:33 + h],
                                                             in1=GO[:, tt, h * 256:(h + 1) * 256], op0=ALU.mult, op1=ALU.mult),
                 reads=[Bnum, Bsmall, BGO[tt]], writes=[Bytok])
        for g4 in range(4):
            b = g4 % 2
            for j in range(4):
                fc = g4 * 4 + j
                P.op("pe", lambda E, b=b, j=j, fc=fc: E.transpose(out=psb[b][:, j * 128:(j + 1) * 128], in_=ytok[:, fc * 128:(fc + 1) * 128], identity=ident),
                     reads=[Bytok, Bconst], writes=[Bps[b]], signal=(j == 3))
            P.op("act", lambda E, b=b, g4=g4: E.copy(out=big[:, 16 + g4 * 4:16 + g4 * 4 + 4, tt * 128:(tt + 1) * 128],
                                                     in_=psb[b][:, 0:512].rearrange("p (j t) -> p j t", j=4)),
                 reads=[Bps[b]], writes=[Bbig[16 + g4 * 4 + j] for j in range(4)])

    for h in range(H):
        P.op("dve", lambda E, h=h: E.memset(C[:, h, :], 0.0), writes=[BC[h]])
    for tb in range(NBLK):
        load_x_and_norm1(tb)
        def gate_epi(fg, tt, b, w):
            P.op("dve", lambda E: E.tensor_tensor(out=gpre[:, tt, :], in0=ps[b][:, 0:16], in1=bif[:, :], op=ALU.add),
                 reads=[Bps[b], Bconst], writes=[Bgates])
        gemm_tok(w_in, D, O_G, 16, hnT_lhs, lambda tt: [BhnT[tt]], NTT, gate_epi)
        if tb == 0:
            dump("hnT_a", hnT[:, 0:16, :], [128, 16, TBX], BF16)
            dump("gpre", gpre[:], [128, NTT, 16])
        gates_block(tb)
        P.barrier()
        gemm_feat(w_in, D, O_K, 1024, hnT_rhs, BhnT, conv_silu_epi(kTb, Bk, 8), halo=True)
        P.barrier()
        def v_epi(fg, tt, b, w):
            P.op("act", lambda E: E.copy(out=vb[:, tt, fg * 512:(fg + 1) * 512], in_=ps[b][:, :]), reads=[Bps[b]], writes=[Bv[tt]])
        gemm_tok(w_in, D, O_V, 2048, hnT_lhs, lambda tt: [BhnT[tt]], NTT, v_epi)
        if tb == 0:
            dump("gates_A", A_[:], [128, 8, 8])
            dump("gates_EB", EB[:], [128, 8, 8])
            dump("gates_EG", EG[:], [128, 8, 8])
            dump("kT", accb[:, 4096:8192], [128, H * TB], BF16)
            dump("v", accb[:, 8192:16384], [128, NTT * 2048], BF16)
        P.dma("sp", s_scr, lambda E, tb=tb: E.dma_start(out=kT_s[tb], in_=accb[:, 4096:8192]), reads=Bk, writes=[Bscr_k[tb]])
        P.dma("sp", s_scr, lambda E, tb=tb: E.dma_start(out=v_s[tb], in_=accb[:, 8192:16384]), reads=Bv, writes=[Bscr_v[tb]])
        for tt in range(NTT):
            state_update(tb * NTT + tt, tt, True)
    P.barrier()
    dump("C_local", C[:], [128, 8, 257])
    et = small[:, 40:48]
    P.op("dve", lambda E: E.tensor_copy(out=et, in_=EG[:, 0, :]), reads=[Bgates], writes=[Bsmall])
    for c in range(1, 8):
        P.op("dve", lambda E, c=c: E.tensor_tensor(out=et, in0=et, in1=EG[:, c, :], op=ALU.mult), reads=[Bsmall, Bgates], writes=[Bsmall])
    cc_in_ap = cc_in.ap()
    cc_out_ap = cc_out.ap()
    P.dma("sp", s_ld, lambda E: E.dma_start(out=cc_in_ap[:, 0:2056], in_=C[:].rearrange("p h e -> p (h e)")), reads=BC, writes=[Bcc])
    P.dma("sp", s_ld, lambda E: E.dma_start(out=cc_in_ap[:, 2056:CW], in_=et), reads=[Bsmall], writes=[Bcc])
    P.dma("pool", s_cc, lambda E: E.collective_compute("AllGather", ALU.bypass, replica_groups=[list(range(NCORES))],
                                                       ins=[cc_in.ap().opt()], outs=[cc_out.ap().opt()]),
          reads=[Bcc], writes=[Bcc], inc=1)
    P.op("dve", lambda E: E.memset(Rt[:, :, :], 0.0), writes=[BR])
    for h in range(H):
        P.op("dve", lambda E, h=h: E.memset(C[:, h, :], 0.0), reads=[], writes=[BC[h]])
    Cflat = C[:].rearrange("p h e -> p (h e)")
    Rflat = accr[:, 0:2056]
    for j in range(NCORES):
        i = j % 2
        if j < NCORES - 1:
            P.dma("sp", s_cj[i], lambda E, i=i, j=j: E.dma_start(out=Cj[i], in_=cc_out_ap[j * 128:(j + 1) * 128, :]), reads=[Bcc], writes=[BCj[i]])
        P.op("dve", lambda E, j=j: E.scalar_tensor_tensor(out=Cflat, in0=Rflat, scalar=onehot[:, j:j + 1], in1=Cflat, op0=ALU.mult, op1=ALU.add),
             reads=[BR, Bconst] + BC, writes=BC)
        if j < NCORES - 1:
            P.op("dve", lambda E, i=i: E.tensor_tensor(out=Rt[:, :, :], in0=Rt[:, :, :],
                                                      in1=Cj[i][:, 2056:CW].unsqueeze(2).to_broadcast([128, 8, 257]), op=ALU.mult),
                 reads=[BR, BCj[i]], writes=[BR])
            P.op("dve", lambda E, i=i: E.tensor_tensor(out=Rflat, in0=Rflat, in1=Cj[i][:, 0:2056], op=ALU.add), reads=[BR, BCj[i]], writes=[BR])
    for h in range(H):
        P.op("dve", lambda E, h=h: E.tensor_copy(out=Cbf[:, h, :], in_=C[:, h, :]), reads=[BC[h]], writes=[BCbf[h]])
    P.barrier()

    dump("C_in", C[:], [128, 8, 257])
    for tb in range(NBLK):
        load_x_and_norm1(tb)
        P.dma("sp", s_ld, lambda E, tb=tb: E.dma_start(out=accb[:, 4096:8192], in_=kT_s[tb]), reads=[Bscr_k[tb]], writes=Bk)
        P.dma("sp", s_ld, lambda E, tb=tb: E.dma_start(out=accb[:, 8192:16384], in_=v_s[tb]), reads=[Bscr_v[tb]], writes=Bv)
        gemm_feat(w_in, D, O_Q, 1024, hnT_rhs, BhnT, conv_silu_epi(qT, Bq, 0), halo=True)
        P.barrier()
        def o_epi(fg, tt, b, w):
            i = tt % 2
            if tt == 0:
                pass
            P.op("act", lambda E: E.activation(out=gt1[i], in_=ps[b][:, :], func=AF.Sigmoid), reads=[Bps[b]], writes=[Bg1[i]])
            P.op("dve", lambda E: E.tensor_tensor(out=GO[:, tt, fg * 512:(fg + 1) * 512], in0=gt1[i], in1=bc512[fg % 2], op=ALU.mult),
                 reads=[Bg1[i], Bbc[fg % 2]], writes=[BGO[tt]])
        for fg in range(4):
            P.dma("sp", s_bc[fg % 2], lambda E, fg=fg: E.dma_start(out=bc512[fg % 2], in_=ghead_in[fg * 512:(fg + 1) * 512].partition_broadcast(128)),
                  writes=[Bbc[fg % 2]])
            gemm_tok(w_in, D, O_O + fg * 512, 512, hnT_lhs, lambda tt: [BhnT[tt]], NTT,
                     (lambda fg: (lambda f0, tt, b, w: o_epi(fg, tt, b, w)))(fg))
        P.barrier()
        for tt in range(NTT):
            c = tb * NTT + tt
            chunk_output(c, tt)
            state_update(c, tt, False)
        P.barrier()
        if tb == 0:
            dump("qT", accb[:, 0:4096], [128, H * TB], BF16)
            dump("GO", accb[:, 16384:24576], [128, NTT * 2048], BF16)
            dump("mix_mlstm", big[:, 16:32, :], [128, 16, TB], BF16)
        for g in range(4):
            wdw = 2 ** (g + 1)
            def pool_epi(fg, fc, b, g=g, wdw=wdw):
                e = fc % 2
                P.op("act", lambda E: E.copy(out=Et[e][:, HALO:TBX], in_=ps[b][:, :]), reads=[Bps[b]], writes=[BE[e]])
                P.op("act", lambda E: E.copy(out=Et[e][:, 0:HALO], in_=ps[4 + fc][:, 0:HALO]), reads=[Bps[4 + fc]], writes=[BE[e]])
                src = Et[e]
                bufs = [(slev[e], Bsl[e]), (pwin[e], Brt[e])]
                cur, Bcur = src, BE[e]
                k = 1
                lo = 0
                n = 0
                while k < wdw:
                    dst, Bd = bufs[n % 2]
                    lo2 = lo + k
                    P.op("dve", lambda E, dst=dst, cur=cur, lo2=lo2, k=k: E.tensor_tensor(out=dst[:, lo2:TBX], in0=cur[:, lo2:TBX], in1=cur[:, lo2 - k:TBX - k], op=ALU.add),
                         reads=[Bcur], writes=[Bd])
                    cur, Bcur = dst, Bd
                    lo = lo2
                    k *= 2
                    n += 1
                P.op("dve", lambda E, cur=cur: E.scalar_tensor_tensor(out=zT[:, fc, :], in0=cur[:, HALO:TBX], scalar=1.0 / wdw,
                                                                     in1=src[:, HALO:TBX], op0=ALU.mult, op1=ALU.subtract),
                     reads=[Bcur, BE[e]], writes=[Bz[fc]])
                P.op("dve", lambda E, cur=cur, tb=tb: E.tensor_tensor(out=ptmp[:, e, :], in0=cur[:, HALO:HALO + 16], in1=icnt[:, tb, g, :], op=ALU.mult),
                     reads=[Bcur, Bconst], writes=[Bpt[e]])
                P.op("dve", lambda E: E.tensor_tensor(out=zT[:, fc, 0:16], in0=ptmp[:, e, :], in1=src[:, HALO:HALO + 16], op=ALU.subtract),
                     reads=[Bpt[e], BE[e], Bz[fc]], writes=[Bz[fc]])
            zT = accb[:, 2 * 12288:2 * 12288 + 4 * TB].rearrange("p (c t) -> p c t", c=4)
            pwin = [accr[:, 14336 + i * 528:14336 + (i + 1) * 528] for i in range(2)]
            Bz = [Buf() for _ in range(4)]
            Bpt = [Buf(), Buf()]
            gemm_feat(w_in, D, O_POOL + g * 512, 512, hnT_rhs, BhnT, pool_epi, halo=True)
            slot = load_block(w_pool, g * 512, 4, 0, 512)
            for fc in range(4):
                b = fc % 2
                for kc in range(4):
                    P.op("pe", lambda E, b=b, fc=fc, kc=kc, slot=slot: E.matmul(out=ps[b][:, :], lhsT=ring[slot][:, kc, fc * 128:(fc + 1) * 128],
                                                                               rhs=zT[:, kc, :], start=(kc == 0), stop=(kc == 3)),
                         reads=[Bring[slot], Bz[kc]], writes=[Bps[b]], signal=(kc == 3))
                ch = g * 4 + fc
                P.op("act", lambda E, b=b, ch=ch: E.activation(out=big[:, ch, :], in_=ps[b][:, :], func=AF.Copy, scale=psc[:, ch:ch + 1]),
                     reads=[Bps[b], Bconst], writes=[Bbig[ch]])
        P.barrier()
        if tb == 0:
            dump("mix_pool", big[:, 0:16, :], [128, 16, TB], BF16)
        r0 = tb * TB + HALO
        for tt in range(NTT):
            P.dma("sp", s_acc[tt], lambda E, tt=tt, r0=r0: E.dma_start(out=acc[tt], in_=x_ext[r0 + tt * 128:r0 + (tt + 1) * 128, :]), writes=[Bacc[tt]])

        def big_lhs(kk, tt):
            return big[:, kk, tt * 128:(tt + 1) * 128]

        def res_epi(fg, tt, b, w):
            P.op("dve", lambda E: E.tensor_tensor(out=acc[tt][:, fg * 512:(fg + 1) * 512], in0=ps[b][:, :], in1=acc[tt][:, fg * 512:(fg + 1) * 512], op=ALU.add),
                 reads=[Bps[b], Bacc[tt]], writes=[Bacc[tt]])
        gemm_tok(w_out, D, 0, D, big_lhs, lambda tt: Bbig, NTT, res_epi)
        if tb == 0:
            dump("h1_0", accr[:, 0:4096], [128, 4096])
            dump("h1_1", accr[:, 4096:8192], [128, 4096])
            dump("h1_2", accr[:, 8192:12288], [128, 4096])
            dump("h1_3", accr[:, 12288:16384], [128, 4096])
        for tt in range(NTT):
            norm_T(acc[tt], Bacc[tt], 128, 1, HALO + tt * 128, BhnT[tt])
        for hc in range(DFF // D):
            def up_epi(fg, fc, b):
                i = fc % 2
                P.op("dve", lambda E: E.tensor_scalar_max(out=rtmp[i], in0=ps[b][:, :], scalar1=0.0), reads=[Bps[b]], writes=[Brt[i]])
                P.op("act", lambda E: E.activation(out=big[:, fg * 4 + fc, :], in_=rtmp[i], func=AF.Square), reads=[Brt[i]], writes=[Bbig[fg * 4 + fc]])
            gemm_feat(w_up, D, hc * D, D, hnT_rhs, BhnT, up_epi, halo=False)
            gemm_tok(w_down[hc * D:(hc + 1) * D, :], D, 0, D, big_lhs, lambda tt: Bbig, NTT, res_epi)
        if tb == 0:
            dump("h2_0", accr[:, 0:4096], [128, 4096])
            dump("h2_1", accr[:, 4096:8192], [128, 4096])
            dump("h2_2", accr[:, 8192:12288], [128, 4096])
            dump("h2_3", accr[:, 12288:16384], [128, 4096])
        for tt in range(NTT):
            norm_T(acc[tt], Bacc[tt], 128, 2, HALO + tt * 128, BhnT[tt])
        for tt in range(NTT):
            rr = tb * TB + tt * 128
            P.dma("sp", s_p, lambda E, rr=rr: E.dma_start(out=pst, in_=p_in[rr:rr + 128, :]), writes=[Bp])
            P.op("dve", lambda E: E.tensor_copy(out=pbf, in_=pst), reads=[Bp], writes=[Bp])
            for j in range(2):
                P.op("pe", lambda E, j=j: E.transpose(out=psb[7][:, j * 128:(j + 1) * 128], in_=pbf[:, j * 128:(j + 1) * 128], identity=ident),
                     reads=[Bp, Bconst], writes=[Bps[7]], signal=(j == 1))
            P.op("act", lambda E, tt=tt: E.copy(out=pT[:, :, tt * 128:(tt + 1) * 128], in_=psb[7][:, 0:256].rearrange("p (j t) -> p j t", j=2)),
                 reads=[Bps[7]], writes=[BpT])
        P.barrier()
        for fg in range(D // 512):
            P.dma("sp", s_bc[fg % 2], lambda E, fg=fg: E.dma_start(out=bc512[fg % 2], in_=bgate_in[fg * 512:(fg + 1) * 512].partition_broadcast(128)),
                  writes=[Bbc[fg % 2]])
            st_gate = setctr["i"] % 2
            st_ple = 1 - st_gate

            def ple_epi(f0, tt, b, w, fg=fg, st_ple=st_ple):
                i = tt % 2
                bp = st_ple * 4 + tt
                P.op("dve", lambda E: E.tensor_tensor(out=gt1[i], in0=ps[b][:, :], in1=bc512[fg % 2], op=ALU.add), reads=[Bps[b], Bbc[fg % 2]], writes=[Bg1[i]])
                P.op("act", lambda E: E.activation(out=gt1[i], in_=gt1[i], func=AF.Sigmoid), reads=[Bg1[i]], writes=[Bg1[i]])
                P.op("dve", lambda E: E.tensor_tensor(out=gt2[i], in0=ps[bp][:, :], in1=gt1[i], op=ALU.mult), reads=[Bps[bp], Bg1[i]], writes=[Bg2[i]])
                P.op("dve", lambda E: E.tensor_tensor(out=acc[tt][:, fg * 512:(fg + 1) * 512], in0=gt2[i], in1=acc[tt][:, fg * 512:(fg + 1) * 512], op=ALU.add),
                     reads=[Bg2[i], Bacc[tt]], writes=[Bacc[tt]])
            slotp = load_block(w_ple, 0, 2, fg * 512, 512)
            for tt in range(NTT):
                bp = st_ple * 4 + tt
                for kc in range(2):
                    P.op("pe", lambda E, bp=bp, tt=tt, kc=kc, slotp=slotp: E.matmul(out=ps[bp][:, :], lhsT=pT[:, kc, tt * 128:(tt + 1) * 128],
                                                                                   rhs=ring[slotp][:, kc, :], start=(kc == 0), stop=(kc == 1)),
                         reads=[Bring[slotp], BpT], writes=[Bps[bp]], signal=(kc == 1))
            gemm_tok(w_gate, D, fg * 512, 512, hnT_lhs, lambda tt: [BhnT[tt]], NTT, ple_epi)
        P.barrier()
        if tb == 0:
            dump("h3_0", accr[:, 0:4096], [128, 4096])
            dump("h3_1", accr[:, 4096:8192], [128, 4096])
            dump("h3_2", accr[:, 8192:12288], [128, 4096])
            dump("h3_3", accr[:, 12288:16384], [128, 4096])
        P.dma("sp", s_ld, lambda E: E.dma_start(out=gfin, in_=gfin_in.partition_broadcast(128)), writes=[Bgfin] + BhnT)
        for tt in range(NTT):
            P.op("dve", lambda E: E.memset(small[:, 0:1], 0.0), writes=[Bsmall])
            P.op("act", lambda E, tt=tt: E.activation(out=hn[:, :], in_=acc[tt], func=AF.Square, accum_out=small[:, 0:1]),
                 reads=[Bacc[tt]], writes=[Bhn, Bsmall])
            rstd_from_ss(0, 128, 1.0 / D)
            P.op("dve", lambda E, tt=tt: E.scalar_tensor_tensor(out=acc[tt], in0=acc[tt], scalar=small[:, 0:1], in1=gfin, op0=ALU.mult, op1=ALU.mult),
                 reads=[Bacc[tt], Bsmall, Bgfin], writes=[Bacc[tt]])
            rr = tb * TB + tt * 128
            P.dma("sp", s_out, lambda E, tt=tt, rr=rr: E.dma_start(out=out[rr:rr + 128, :], in_=acc[tt]), reads=[Bacc[tt]])
        P.barrier()
    P.wait_event("sp", (s_out, P.cnt[s_out]))
    P.run()
    P.close()
    es.close()
    nc._dbg_names = dbg_names
    return nc


def _host_consts():
    s = np.arange(128)
    U = (s[:, None] <= s[None, :]).astype(np.float32)
    sel = np.zeros((128, 128), np.float32)
    sel[127, :] = 1.0
    cst = np.stack([U, sel, np.zeros((128, 128), np.float32)], axis=1)
    ident = np.eye(128, dtype=np.float32)
    cstb = np.stack([ident, U], axis=1).astype(ml_dtypes.bfloat16)
    return np.ascontiguousarray(cst), np.ascontiguousarray(cstb)


_NC_CACHE = {}


def kernel(x, p, g_mix, w_in, conv_w, conv_b, b_igate, b_fgate, g_head, w_pool, pool_scale,
           w_out, g_mlp, w_up, w_down, g_ple, w_ple_gate, b_ple_gate, w_ple, g_final):
    f32 = np.float32
    x = np.asarray(x, f32)[0]
    p = np.asarray(p, f32)[0, 0]
    cst, cstb = _host_consts()

    def fm(v, n):
        return np.ascontiguousarray(np.asarray(v, f32).reshape(n, 128).T)

    gT = np.concatenate([fm(g_mix[0], KC), fm(g_mlp[0], KC), fm(g_ple[0], KC)], axis=1)
    cwv = np.asarray(conv_w, f32)[0]
    cbv = np.asarray(conv_b, f32)[0]
    cw = np.stack([fm(cwv[0], 16), fm(cwv[1], 16), fm(cwv[2], 16), fm(cwv[3], 16), fm(cbv, 16)], axis=2)
    psc = fm(pool_scale[0], 16)
    bif = np.ascontiguousarray(np.tile(np.concatenate([np.asarray(b_igate, f32)[0], np.asarray(b_fgate, f32)[0]])[None, :], (128, 1)))
    common = {
        "w_in": np.asarray(w_in, f32)[0], "w_pool": np.asarray(w_pool, f32)[0].reshape(4 * 512, 512),
        "w_out": np.asarray(w_out, f32)[0], "w_up": np.asarray(w_up, f32)[0], "w_down": np.asarray(w_down, f32)[0],
        "w_gate": np.asarray(w_ple_gate, f32)[0], "w_ple": np.asarray(w_ple, f32)[0],
        "gT": gT, "cw": np.ascontiguousarray(cw), "psc": psc, "bif": bif,
        "g_head": np.asarray(g_head, f32)[0], "b_gate": np.asarray(b_ple_gate, f32)[0], "g_final": np.asarray(g_final, f32),
        "cst": cst, "cstb": cstb,
    }
    in_maps = []
    windows = (2, 4, 8, 16)
    for c in range(NCORES):
        xe = np.zeros((T + HALO, D), f32)
        xe[HALO:] = x[c * T:(c + 1) * T]
        if c > 0:
            xe[:HALO] = x[c * T - HALO:c * T]
        onehot = np.zeros((128, 8), f32)
        onehot[:, c] = 1.0
        icnt = np.zeros((128, NBLK, 4, 16), f32)
        for tb in range(NBLK):
            tg = c * T + tb * TB + np.arange(16)
            for gi, wdw in enumerate(windows):
                icnt[:, tb, gi, :] = (1.0 / np.minimum(tg + 1, wdw)).astype(f32)[None, :]
        m = dict(common)
        m.update({"x_ext": xe, "p": np.ascontiguousarray(p[c * T:(c + 1) * T]), "onehot": onehot, "icnt": icnt})
        in_maps.append(m)
    if "nc" not in _NC_CACHE:
        _NC_CACHE["nc"] = build_nc()
    res = run_bass_kernel_spmd(_NC_CACHE["nc"], in_maps, core_ids=list(range(NCORES)))
    if DEBUG:
        _NC_CACHE["dbg"] = [{n: np.asarray(r[n]) for n in _NC_CACHE["nc"]._dbg_names} for r in res.results]
    outp = np.concatenate([np.asarray(r["out"], f32) for r in res.results], axis=0)
    return outp.reshape(1, SEQ, D)
```

```python
import contextlib
import numpy as np
import ml_dtypes
import concourse.bass as bass
import concourse.mybir as mybir
from concourse.bass_utils import run_bass_kernel_spmd

F32 = mybir.dt.float32
BF16 = mybir.dt.bfloat16
AF = mybir.ActivationFunctionType
ALU = mybir.AluOpType

NCORES = 8
D = 4096
KC = D // 128
SEQ = 8192
T = SEQ // NCORES
TB = 512
NBLK = T // TB
NTT = TB // 128
HALO = 16
TBX = TB + HALO
H = 8
DQK = 128
DV = 256
DFF = 16384
DPLE = 256
D_IN = 8208
O_POOL, O_Q, O_K, O_V, O_O, O_G = 0, 2048, 3072, 4096, 6144, 8192
EPS = 1e-6
CAP = 15.0
CW = 8 * 257 + 8
RING = 4
BKC = 8
DEBUG = False


class Buf:
    __slots__ = ("w", "r")

    def __init__(self):
        self.w = None
        self.r = []


class Prog:
    ENGS = ("pe", "act", "dve", "pool", "sp")

    def __init__(self, nc):
        self.nc = nc
        self.ops = {e: [] for e in self.ENGS}
        self.cnt = {}
        self.waited = {e: {} for e in self.ENGS}
        self.sems = {}
        self._cm = []
        for e in self.ENGS:
            self.new_sem(("eng", e))

    def new_sem(self, key):
        cm = self.nc.semaphore("s%d" % len(self.sems))
        self.sems[key] = cm.__enter__()
        self._cm.append(cm)
        self.cnt[key] = 0
        return key

    def _deps(self, reads, writes):
        deps = []
        for b in reads:
            if b.w is not None:
                deps.append(b.w)
        for b in writes:
            if b.w is not None:
                deps.append(b.w)
            deps.extend(b.r)
        return deps

    def _emit_waits(self, eng, deps, skip_key=None):
        need = {}
        for (k, v) in deps:
            if k == skip_key:
                continue
            if self.waited[eng].get(k, 0) >= v:
                continue
            if need.get(k, 0) < v:
                need[k] = v
        for k, v in need.items():
            self.waited[eng][k] = v
            h = self.sems[k]
            self.ops[eng].append(lambda E, h=h, v=v: E.wait_ge(h, v))

    def _mark(self, ev, reads, writes):
        for b in reads:
            b.r.append(ev)
            if len(b.r) > 24:
                best = {}
                for (k, v) in b.r:
                    if best.get(k, 0) < v:
                        best[k] = v
                b.r = list(best.items())
        for b in writes:
            b.w = ev
            b.r = []

    def op(self, eng, fn, reads=(), writes=(), signal=True):
        deps = self._deps(reads, writes)
        key = ("eng", eng)
        self._emit_waits(eng, deps, skip_key=(key if eng == "pe" else None))
        if signal:
            self.cnt[key] += 1
            h = self.sems[key]
            self.ops[eng].append(lambda E, fn=fn, h=h: fn(E).then_inc(h, 1))
            ev = (key, self.cnt[key])
        else:
            self.ops[eng].append(lambda E, fn=fn: fn(E))
            ev = (key, self.cnt[key] + 1)
        self._mark(ev, reads, writes)
        return ev

    def dma(self, eng, semkey, fn, reads=(), writes=(), inc=16):
        deps = self._deps(reads, writes)
        self._emit_waits(eng, deps)
        self.cnt[semkey] += inc
        h = self.sems[semkey]
        self.ops[eng].append(lambda E, fn=fn, h=h, inc=inc: fn(E).then_inc(h, inc))
        ev = (semkey, self.cnt[semkey])
        self._mark(ev, reads, writes)
        return ev

    def barrier(self, engs=("pe", "act", "dve", "sp")):
        evs = [(k, v) for k, v in self.cnt.items()
               if v > 0 and not (isinstance(k, tuple) and k[0] == "ring") and k != ("eng", "pool") and k != "cc"]
        for e in engs:
            self._emit_waits(e, evs, skip_key=None)

    def wait_event(self, eng, ev):
        self._emit_waits(eng, [ev])

    def run(self):
        with self.nc.Block() as block:
            @block.tensor
            def _(E):
                for f in self.ops["pe"]:
                    f(E)

            @block.scalar
            def _(E):
                for f in self.ops["act"]:
                    f(E)

            @block.vector
            def _(E):
                for f in self.ops["dve"]:
                    f(E)

            @block.gpsimd
            def _(E):
                for f in self.ops["pool"]:
                    f(E)

            @block.sync
            def _(E):
                for f in self.ops["sp"]:
                    f(E)

    def close(self):
        for cm in reversed(self._cm):
            cm.__exit__(None, None, None)


def build_nc():
    nc = bass.Bass("TRN2", target_bir_lowering=False)
    es = contextlib.ExitStack()

    def din(name, shape, dt=F32):
        return nc.dram_tensor(name, list(shape), dt, kind="ExternalInput").ap()

    x_ext = din("x_ext", [T + HALO, D])
    p_in = din("p", [T, DPLE])
    w_in = din("w_in", [D, D_IN])
    w_pool = din("w_pool", [4 * 512, 512])
    w_out = din("w_out", [D, D])
    w_up = din("w_up", [D, DFF])
    w_down = din("w_down", [DFF, D])
    w_gate = din("w_gate", [D, D])
    w_ple = din("w_ple", [DPLE, D])
    gT_in = din("gT", [128, 3 * KC])
    cw_in = din("cw", [128, 16, 5])
    psc_in = din("psc", [128, 16])
    bif_in = din("bif", [128, 16])
    ghead_in = din("g_head", [2048])
    bgate_in = din("b_gate", [D])
    gfin_in = din("g_final", [D])
    onehot_in = din("onehot", [128, 8])
    icnt_in = din("icnt", [128, NBLK, 4, 16])
    cst_in = din("cst", [128, 3, 128])
    cstb_in = din("cstb", [128, 2, 128], BF16)
    out = nc.dram_tensor("out", [T, D], F32, kind="ExternalOutput").ap()
    kT_s = nc.dram_tensor("kT_s", [NBLK, 128, H * TB], BF16).ap()
    v_s = nc.dram_tensor("v_s", [NBLK, 128, NTT * 2048], BF16).ap()
    cc_in = nc.dram_tensor("cc_in", [128, CW], F32)
    cc_out = nc.dram_tensor("cc_out", [NCORES * 128, CW], F32)

    P = Prog(nc)
    dbg_names = []

    def dump(name, src, shape, dt=F32):
        if not DEBUG:
            return
        import os
        sel = os.environ.get("DBGSEL", "")
        if sel and not any(name.startswith(x) for x in sel.split(",")):
            return
        P.barrier()
        t = nc.dram_tensor("dbg_" + name, list(shape), dt, kind="ExternalOutput").ap()
        dbg_names.append("dbg_" + name)
        P.dma("sp", s_ld, lambda E: E.dma_start(out=t, in_=src))
        P.barrier()

    def sb(name, shape, dt):
        return es.enter_context(nc.sbuf_tensor("sb_" + name, list(shape), dt))

    cst = sb("cst", [128, 3, 128], F32)
    cstb = sb("cstb", [128, 2, 128], BF16)
    gT = sb("gT", [128, 3 * KC], F32)
    cw = sb("cw", [128, 16, 5], F32)
    psc = sb("psc", [128, 16], F32)
    bif = sb("bif", [128, 16], F32)
    onehot = sb("onehot", [128, 8], F32)
    icnt = sb("icnt", [128, NBLK, 4, 16], F32)
    onesb = sb("onesb", [128, 2], BF16)
    A_ = sb("A_", [128, 8, 8], F32)
    Abf = sb("Abf", [128, 8, 8], BF16)
    EB = sb("EB", [128, 8, 8], F32)
    EG = sb("EG", [128, 8, 8], F32)
    C = sb("C", [128, 8, 257], F32)
    Cbf = sb("Cbf", [128, 8, 257], BF16)
    small = sb("small", [128, 64], F32)
    gpre = sb("gpre", [128, NTT, 16], F32)
    gtmp = sb("gtmp", [128, 6, NTT * 16], F32)
    ptmp = sb("ptmp", [128, 2, 16], F32)
    hn = sb("hn", [128, D], BF16)
    accr = sb("accr", [128, 16384], F32)
    hnT = sb("hnT", [128, KC, TBX], BF16)
    big = sb("big", [128, KC, TB], BF16)
    ring = [sb("ring%d" % i, [128, BKC, 512], BF16) for i in range(RING)]
    miscu = sb("miscu", [128, 4480], F32)

    acc = [accr[:, tt * D:(tt + 1) * D] for tt in range(NTT)]
    accb = accr[:].bitcast(BF16)
    qT = accb[:, 0:4096].rearrange("p (h t) -> p h t", h=H)
    kTb = accb[:, 4096:8192].rearrange("p (h t) -> p h t", h=H)
    vb = accb[:, 8192:16384].rearrange("p (a e) -> p a e", a=NTT)
    GO = accb[:, 16384:24576].rearrange("p (a e) -> p a e", a=NTT)
    numsb = accr[:, 12288:12288 + 2056].rearrange("p (h e) -> p h e", h=8)
    ytok = accb[:, 2 * (12288 + 2056):2 * (12288 + 2056) + 2048]
    Rt = accr[:, 0:2056].rearrange("p (h e) -> p h e", h=8)
    Cj = [accr[:, 4096 + i * 2304:4096 + i * 2304 + CW] for i in range(2)]
    bigf = big[:].rearrange("p k t -> p (k t)").bitcast(F32)
    xst = [bigf[:, i * D:(i + 1) * D] for i in range(2)]
    gfin = hnT[:].rearrange("p k t -> p (k t)").bitcast(F32)[:, 0:D]
    mub = miscu[:].bitcast(BF16)
    Et = [miscu[:, i * 528:(i + 1) * 528] for i in range(4)]
    slev = [miscu[:, 2112 + i * 528:2112 + (i + 1) * 528] for i in range(2)]
    av = [mub[:, i * 2048:(i + 1) * 2048].rearrange("p (h e) -> p h e", h=8) for i in range(2)]
    ktok = [mub[:, 4096 + i * 128:4096 + (i + 1) * 128] for i in range(2)]
    WT = [mub[:, 4352 + i * 128:4352 + (i + 1) * 128] for i in range(2)]
    cuet = [miscu[:, 2304 + i * 264:2304 + i * 264 + 257] for i in range(2)]
    sqj = miscu[:, 2832:3088]
    rtmp = [miscu[:, i * 512:(i + 1) * 512] for i in range(2)]
    gt1 = [miscu[:, 1024 + i * 512:1024 + (i + 1) * 512] for i in range(2)]
    gt2 = [miscu[:, 2048 + i * 512:2048 + (i + 1) * 512] for i in range(2)]
    bc512 = [miscu[:, 3072 + i * 512:3072 + (i + 1) * 512] for i in range(2)]
    pst = miscu[:, 4096:4352]
    pbf = mub[:, 8704:8960]
    pT = sb("pT", [128, 2, TB], BF16)

    ps = [es.enter_context(nc.psum_tensor("ps%d" % i, [128, 512], F32)) for i in range(8)]
    psb = [t[:].bitcast(BF16) for t in ps]

    ident = cstb[:, 0, :]
    mask01 = cstb[:, 1, :]
    Umat = cst[:, 0, :]
    sel127 = cst[:, 1, :]

    Bps = [Buf() for _ in range(8)]
    Bring = [Buf() for _ in range(RING)]
    Bconst = Buf()
    Bhn = Buf()
    BhnT = [Buf() for _ in range(NTT + 1)]
    Bacc = [Buf() for _ in range(NTT)]
    Bxst = [Buf(), Buf()]
    Bsmall = Buf()
    Bgates = Buf()
    BC = [Buf() for _ in range(H)]
    BCbf = [Buf() for _ in range(H)]
    Bq = [Buf() for _ in range(H)]
    Bk = [Buf() for _ in range(H)]
    Bv = [Buf() for _ in range(NTT)]
    BGO = [Buf() for _ in range(NTT)]
    Bbig = [Buf() for _ in range(KC)]
    BE = [Buf() for _ in range(4)]
    Bsl = [Buf(), Buf()]
    Bav = [Buf(), Buf()]
    Bktok = [Buf(), Buf()]
    BWT = [Buf(), Buf()]
    Bnum = Buf()
    Bytok = Buf()
    Brt = [Buf(), Buf()]
    Bg1 = [Buf(), Buf()]
    Bg2 = [Buf(), Buf()]
    Bbc = [Buf(), Buf()]
    Bp = Buf()
    BpT = Buf()
    BR = Buf()
    BCj = [Buf(), Buf()]
    Bgfin = Buf()
    Bscr_k = [Buf() for _ in range(NBLK)]
    Bscr_v = [Buf() for _ in range(NBLK)]
    Bcc = Buf()

    s_ring = [P.new_sem(("ring", i)) for i in range(RING)]
    s_const = P.new_sem("const")
    s_x = [P.new_sem(("x", i)) for i in range(2)]
    s_acc = [P.new_sem(("acc", i)) for i in range(NTT)]
    s_misc = [P.new_sem(("misc", i)) for i in range(2)]
    s_bc = [P.new_sem(("bc", i)) for i in range(2)]
    s_p = P.new_sem("p")
    s_scr = P.new_sem("scr")
    s_ld = P.new_sem("ld")
    s_cc = P.new_sem("cc")
    s_cj = [P.new_sem(("cj", i)) for i in range(2)]
    s_out = P.new_sem("out")

    for (dst, src) in ((cst, cst_in), (cstb, cstb_in), (gT, gT_in), (cw, cw_in), (psc, psc_in),
                       (bif, bif_in), (onehot, onehot_in), (icnt, icnt_in)):
        P.dma("sp", s_const, lambda E, dst=dst, src=src: E.dma_start(out=dst[:], in_=src), writes=[Bconst])
    P.op("dve", lambda E: E.memset(onesb[:], 1.0), writes=[Bconst])

    wstate = {"i": 0}

    def load_block(W2d, k0, nkc, c0, ncols):
        i = wstate["i"]
        wstate["i"] += 1
        slot = i % RING
        src = W2d[k0:k0 + nkc * 128, c0:c0 + ncols].rearrange("(kc p) n -> p kc n", p=128)
        P.dma("pool", s_ring[slot],
              lambda E, slot=slot, src=src, nkc=nkc, ncols=ncols: E.dma_start(out=ring[slot][:, 0:nkc, 0:ncols], in_=src),
              writes=[Bring[slot]])
        return slot

    setctr = {"i": 0}

    def gemm_tok(W2d, K, c0, ncols, lhs_fn, lhs_bufs, ntt, epilogue, extra=None, width=512):
        nkb = (K + BKC * 128 - 1) // (BKC * 128)
        for fg in range((ncols + width - 1) // width):
            w = min(width, ncols - fg * width)
            st = setctr["i"] % 2
            setctr["i"] += 1
            for kb in range(nkb):
                nkc = min(BKC, K // 128 - kb * BKC)
                slot = load_block(W2d, kb * BKC * 128, nkc, c0 + fg * width, w)
                for tt in range(ntt):
                    b = st * 4 + tt
                    for kc in range(nkc):
                        first = (kb == 0 and kc == 0)
                        last = (kb == nkb - 1 and kc == nkc - 1) and extra is None
                        sig = last or (tt == ntt - 1 and kc == nkc - 1)
                        P.op("pe", lambda E, b=b, tt=tt, kk=kb * BKC + kc, kc=kc, slot=slot, w=w, first=first, last=last:
                             E.matmul(out=ps[b][:, 0:w], lhsT=lhs_fn(kk, tt), rhs=ring[slot][:, kc, 0:w], start=first, stop=last),
                             reads=[Bring[slot]] + lhs_bufs(tt), writes=[Bps[b]], signal=sig)
            if extra is not None:
                extra(fg, st)
            for tt in range(ntt):
                epilogue(fg, tt, st * 4 + tt, w)

    def gemm_feat(W2d, K, c0, ncols, rhs_fn, rhs_bufs, epilogue, halo, pre=None):
        nkb = K // (BKC * 128)
        for fg in range(ncols // 512):
            if halo:
                st = 0
            else:
                st = setctr["i"] % 2
                setctr["i"] += 1
            for kb in range(nkb):
                slot = load_block(W2d, kb * BKC * 128, BKC, c0 + fg * 512, 512)
                for fc in range(4):
                    b = st * 4 + fc
                    for kc in range(BKC):
                        first = (kb == 0 and kc == 0)
                        last = (kb == nkb - 1 and kc == BKC - 1)
                        kk = kb * BKC + kc
                        P.op("pe", lambda E, b=b, fc=fc, kk=kk, kc=kc, slot=slot, first=first, last=last:
                             E.matmul(out=ps[b][:, :], lhsT=ring[slot][:, kc, fc * 128:(fc + 1) * 128], rhs=rhs_fn(kk, False),
                                      start=first, stop=last),
                             reads=[Bring[slot]] + rhs_bufs, writes=[Bps[b]], signal=((last or (fc == 3 and kc == BKC - 1)) and not halo))
                        if halo:
                            P.op("pe", lambda E, fc=fc, kk=kk, kc=kc, slot=slot, first=first, last=last:
                                 E.matmul(out=ps[4 + fc][:, 0:HALO], lhsT=ring[slot][:, kc, fc * 128:(fc + 1) * 128],
                                          rhs=rhs_fn(kk, True), start=first, stop=last),
                                 reads=[Bring[slot]] + rhs_bufs, writes=[Bps[4 + fc]], signal=(last or (fc == 3 and kc == BKC - 1)))
            if pre is not None:
                for fc in range(4):
                    pre(fg, fc, st * 4 + fc)
            for fc in range(4):
                epilogue(fg, fc, st * 4 + fc)

    def rstd_from_ss(col, npart, scale):
        c = small[0:npart, col:col + 1]
        P.op("dve", lambda E: E.tensor_scalar(out=c, in0=c, scalar1=scale, scalar2=EPS, op0=ALU.mult, op1=ALU.add),
             reads=[Bsmall], writes=[Bsmall])
        P.op("act", lambda E: E.activation(out=c, in_=c, func=AF.Sqrt), reads=[Bsmall], writes=[Bsmall])
        P.op("dve", lambda E: E.reciprocal(out=c, in_=c), reads=[Bsmall], writes=[Bsmall])

    def norm_T(src, Bsrc, npart, gsel, tcol0, Bdst):
        P.op("dve", lambda E: E.memset(small[0:npart, 0:1], 0.0), writes=[Bsmall])
        P.op("act", lambda E: E.activation(out=hn[0:npart, :], in_=src, func=AF.Square, accum_out=small[0:npart, 0:1]),
             reads=[Bsrc], writes=[Bhn, Bsmall])
        rstd_from_ss(0, npart, 1.0 / D)
        P.op("act", lambda E: E.activation(out=hn[0:npart, :], in_=src, func=AF.Copy, scale=small[0:npart, 0:1]),
             reads=[Bsrc, Bsmall], writes=[Bhn])
        for g4 in range(KC // 4):
            b = g4 % 4
            for j in range(4):
                kc = g4 * 4 + j
                P.op("pe", lambda E, b=b, j=j, kc=kc: E.transpose(out=psb[b][:, j * 128:j * 128 + npart],
                                                                  in_=hn[0:npart, kc * 128:(kc + 1) * 128],
                                                                  identity=ident[0:npart, 0:npart]),
                     reads=[Bhn, Bconst], writes=[Bps[b]], signal=(j == 3))
            eng = "dve" if g4 % 2 == 0 else "dve"
            P.op(eng, lambda E, b=b, g4=g4: E.tensor_tensor(
                out=hnT[:, g4 * 4:g4 * 4 + 4, tcol0:tcol0 + npart],
                in0=psb[b][:, 0:512].rearrange("p (j t) -> p j t", j=4)[:, :, 0:npart],
                in1=gT[:, gsel * KC + g4 * 4:gsel * KC + g4 * 4 + 4].unsqueeze(2).to_broadcast([128, 4, npart]),
                op=ALU.mult), reads=[Bps[b], Bconst], writes=[Bdst])

    def load_x_and_norm1(tb):
        r0 = tb * TB
        P.dma("sp", s_x[0], lambda E: E.dma_start(out=xst[0][0:HALO, :], in_=x_ext[r0:r0 + HALO, :]), writes=[Bxst[0]])
        norm_T(xst[0][0:HALO, :], Bxst[0], HALO, 0, 0, BhnT[NTT])
        for tt in range(NTT):
            i = (tt + 1) % 2
            rr = r0 + HALO + tt * 128
            P.dma("sp", s_x[i], lambda E, i=i, rr=rr: E.dma_start(out=xst[i][:, :], in_=x_ext[rr:rr + 128, :]), writes=[Bxst[i]])
            norm_T(xst[i][:, :], Bxst[i], 128, 0, HALO + tt * 128, BhnT[tt])

    def hnT_rhs(kk, halo):
        return hnT[:, kk, 0:HALO] if halo else hnT[:, kk, HALO:TBX]

    def hnT_lhs(kk, tt):
        return hnT[:, kk, HALO + tt * 128:HALO + (tt + 1) * 128]

    def halo_pre(fg, fc, b):
        P.op("act", lambda E: E.copy(out=Et[fc][:, HALO:TBX], in_=ps[b][:, :]), reads=[Bps[b]], writes=[BE[fc]])
        P.op("act", lambda E: E.copy(out=Et[fc][:, 0:HALO], in_=ps[4 + fc][:, 0:HALO]), reads=[Bps[4 + fc]], writes=[BE[fc]])

    def conv_silu_epi(dstT, Bdst, ch0):
        def epi(fg, fc, b):
            ch = ch0 + fg * 4 + fc
            h = (fg * 4 + fc)
            e = (fg * 4 + fc) % 2
            s = slev[e]
            P.op("dve", lambda E: E.tensor_scalar(out=s[:, 0:TB], in0=Et[fc][:, HALO:TBX], scalar1=cw[:, ch, 3:4],
                                                  scalar2=cw[:, ch, 4:5], op0=ALU.mult, op1=ALU.add),
                 reads=[BE[fc], Bconst], writes=[Bsl[e]])
            for j in range(3):
                P.op("dve", lambda E, j=j: E.scalar_tensor_tensor(out=s[:, 0:TB], in0=Et[fc][:, HALO - 3 + j:HALO - 3 + j + TB],
                                                                 scalar=cw[:, ch, j:j + 1], in1=s[:, 0:TB],
                                                                 op0=ALU.mult, op1=ALU.add),
                     reads=[BE[fc], Bconst, Bsl[e]], writes=[Bsl[e]])
            P.op("act", lambda E: E.activation(out=dstT[:, h, :], in_=s[:, 0:TB], func=AF.Silu), reads=[Bsl[e]], writes=[Bdst[h]])
        return epi

    def gates_block(tb):
        n = NTT * 16
        g2 = gpre[:].rearrange("p a c -> p (a c)")
        t0, t1, t2, t3 = (gtmp[:, i, :] for i in (0, 1, 2, 4))
        P.op("act", lambda E: E.activation(out=t0, in_=g2, func=AF.Tanh, scale=1.0 / CAP), reads=[Bgates], writes=[Bgates])
        P.op("dve", lambda E: E.tensor_scalar_mul(out=t0, in0=t0, scalar1=CAP), reads=[Bgates], writes=[Bgates])
        t0v = t0.rearrange("p (a c) -> p a c", a=NTT)
        li = t0v[:, :, 0:8]
        fp = t0v[:, :, 8:16]
        t1v = t1.rearrange("p (a c) -> p a c", a=NTT)[:, :, 0:8]
        t2v = t2.rearrange("p (a c) -> p a c", a=NTT)[:, :, 0:8]
        t3v = t3.rearrange("p (a c) -> p a c", a=NTT)[:, :, 0:8]
        P.op("act", lambda E: E.activation(out=t1v, in_=fp, func=AF.Exp, scale=-1.0), reads=[Bgates], writes=[Bgates])
        P.op("act", lambda E: E.activation(out=t1v, in_=t1v, func=AF.Ln, bias=1.0), reads=[Bgates], writes=[Bgates])
        lc = gtmp[:, 2, 0:NTT * 8].rearrange("p (a c) -> p a c", a=NTT)
        P.op("dve", lambda E: E.tensor_copy(out=lc, in_=t1v), reads=[Bgates], writes=[Bgates])
        lcf = gtmp[:, 2, 0:NTT * 8]
        P.op("pe", lambda E: E.matmul(out=ps[5][:, 0:NTT * 8], lhsT=Umat, rhs=lcf, start=True, stop=True),
             reads=[Bgates, Bconst], writes=[Bps[5]])
        cum = gtmp[:, 3, 0:NTT * 8]
        P.op("dve", lambda E: E.tensor_copy(out=cum, in_=ps[5][:, 0:NTT * 8]), reads=[Bps[5]], writes=[Bgates])
        cumv = cum.rearrange("p (a c) -> p a c", a=NTT)
        c0 = tb * NTT
        P.op("act", lambda E: E.activation(out=EB[:, c0:c0 + NTT, :], in_=cumv, func=AF.Exp, scale=-1.0), reads=[Bgates], writes=[Bgates])
        P.op("dve", lambda E: E.tensor_tensor(out=t3v, in0=li, in1=cumv, op=ALU.add), reads=[Bgates], writes=[Bgates])
        P.op("act", lambda E: E.activation(out=A_[:, c0:c0 + NTT, :], in_=t3v, func=AF.Exp), reads=[Bgates], writes=[Bgates])
        P.op("dve", lambda E: E.tensor_scalar_mul(out=A_[:, c0:c0 + NTT, :], in0=A_[:, c0:c0 + NTT, :], scalar1=float(DQK ** -0.5)), reads=[Bgates], writes=[Bgates])
        P.op("dve", lambda E: E.tensor_copy(out=Abf[:, c0:c0 + NTT, :], in_=A_[:, c0:c0 + NTT, :]), reads=[Bgates], writes=[Bgates])
        P.op("pe", lambda E: E.matmul(out=ps[6][:, 0:NTT * 8], lhsT=sel127, rhs=cum, start=True, stop=True),
             reads=[Bgates, Bconst], writes=[Bps[6]])
        P.op("act", lambda E: E.activation(out=EG[:, c0:c0 + NTT, :], in_=ps[6][:, 0:NTT * 8].rearrange("p (a c) -> p a c", a=NTT),
                                           func=AF.Exp, scale=-1.0), reads=[Bps[6]], writes=[Bgates])

    def state_update(c, tt, first_av):
        ai = c % 2
        P.op("dve", lambda E: E.tensor_tensor(out=av[ai][:, :, :], in0=vb[:, tt, :].rearrange("p (h e) -> p h e", h=8),
                                              in1=A_[:, c, :].unsqueeze(2).to_broadcast([128, 8, 256]), op=ALU.mult),
             reads=[Bv[tt], Bgates], writes=[Bav[ai]])
        for h in range(H):
            ki = h % 2
            pb = 6 + (h % 2)
            P.op("pe", lambda E, h=h, pb=pb: E.transpose(out=psb[pb][:, 768:896], in_=kTb[:, h, tt * 128:(tt + 1) * 128], identity=ident),
                 reads=[Bk[h], Bconst], writes=[Bps[pb]])
            P.op("act", lambda E, ki=ki, pb=pb: E.copy(out=ktok[ki], in_=psb[pb][:, 768:896]), reads=[Bps[pb]], writes=[Bktok[ki]])
            P.op("pe", lambda E, h=h, ki=ki, pb=pb: E.matmul(out=ps[pb][:, 0:256], lhsT=ktok[ki], rhs=av[ai][:, h, :], start=True, stop=True),
                 reads=[Bktok[ki], Bav[ai]], writes=[Bps[pb]], signal=False)
            P.op("pe", lambda E, h=h, ki=ki, pb=pb: E.matmul(out=ps[pb][:, 256:257], lhsT=ktok[ki], rhs=Abf[:, c, h:h + 1], start=True, stop=True),
                 reads=[Bktok[ki], Bgates], writes=[Bps[pb]])
            cue = cuet[h % 2]
            P.op("act", lambda E, h=h, pb=pb, cue=cue: E.activation(out=cue, in_=ps[pb][:, 0:257], func=AF.Copy, scale=EG[:, c, h:h + 1]),
                 reads=[Bps[pb], Bgates], writes=[Bg1[h % 2]])
            P.op("dve", lambda E, h=h, cue=cue: E.scalar_tensor_tensor(out=C[:, h, :], in0=C[:, h, :], scalar=EG[:, c, h:h + 1], in1=cue,
                                                                      op0=ALU.mult, op1=ALU.add),
                 reads=[BC[h], Bg1[h % 2], Bgates], writes=[BC[h]])
            P.op("dve", lambda E, h=h: E.tensor_copy(out=Cbf[:, h, :], in_=C[:, h, :]), reads=[BC[h]], writes=[BCbf[h]])

    def chunk_output(c, tt):
        P.op("dve", lambda E: E.memset(small[:, 8:16], 0.0), writes=[Bsmall])
        for h in range(H):
            wi = h % 2
            pb = 4 + (h % 2)
            P.op("pe", lambda E, h=h, wi=wi: E.matmul(out=ps[2 + wi][:, 0:128], lhsT=kTb[:, h, tt * 128:(tt + 1) * 128],
                                               rhs=qT[:, h, tt * 128:(tt + 1) * 128], start=True, stop=True),
                 reads=[Bk[h], Bq[h]], writes=[Bps[2 + wi]])
            P.op("dve", lambda E, h=h, wi=wi: E.scalar_tensor_tensor(out=WT[wi], in0=ps[2 + wi][:, 0:128], scalar=A_[:, c, h:h + 1],
                                                                    in1=mask01, op0=ALU.mult, op1=ALU.mult),
                 reads=[Bps[2 + wi], Bgates, Bconst], writes=[BWT[wi]])
            P.op("pe", lambda E, h=h, wi=wi, pb=pb: E.matmul(out=ps[pb][:, 0:256], lhsT=WT[wi], rhs=vb[:, tt, h * 256:(h + 1) * 256],
                                                             start=True, stop=False),
                 reads=[BWT[wi], Bv[tt]], writes=[Bps[pb]], signal=False)
            P.op("pe", lambda E, h=h, pb=pb: E.matmul(out=ps[pb][:, 0:256], lhsT=qT[:, h, tt * 128:(tt + 1) * 128], rhs=Cbf[:, h, 0:256],
                                                      start=False, stop=True),
                 reads=[Bq[h], BCbf[h]], writes=[Bps[pb]], signal=False)
            P.op("pe", lambda E, wi=wi, pb=pb: E.matmul(out=ps[pb][:, 256:257], lhsT=WT[wi], rhs=onesb[:, 0:1], start=True, stop=False),
                 reads=[BWT[wi], Bconst], writes=[Bps[pb]], signal=False)
            P.op("pe", lambda E, h=h, pb=pb: E.matmul(out=ps[pb][:, 256:257], lhsT=qT[:, h, tt * 128:(tt + 1) * 128], rhs=Cbf[:, h, 256:257],
                                                      start=False, stop=True),
                 reads=[Bq[h], BCbf[h]], writes=[Bps[pb]])
            P.op("act", lambda E, h=h, pb=pb: E.copy(out=numsb[:, h, :], in_=ps[pb][:, 0:257]), reads=[Bps[pb]], writes=[Bnum])
            P.op("act", lambda E, h=h: E.activation(out=sqj, in_=numsb[:, h, 0:256], func=AF.Square,
                                                    accum_out=small[:, 8 + h:9 + h]),
                 reads=[Bnum], writes=[Bg2[0], Bsmall])
        ssq = small[:, 8:16]
        den = small[:, 16:24]
        r = small[:, 24:32]
        t = small[:, 32:40]
        ebc = EB[:, c, :]
        P.op("dve", lambda E: E.tensor_tensor(out=den, in0=numsb[:, :, 256], in1=ebc, op=ALU.mult), reads=[Bnum, Bgates], writes=[Bsmall])
        P.op("dve", lambda E: E.tensor_scalar_mul(out=t, in0=den, scalar1=-1.0), reads=[Bsmall], writes=[Bsmall])
        P.op("dve", lambda E: E.tensor_tensor(out=den, in0=den, in1=t, op=ALU.max), reads=[Bsmall], writes=[Bsmall])
        P.op("dve", lambda E: E.tensor_scalar_max(out=den, in0=den, scalar1=1.0), reads=[Bsmall], writes=[Bsmall])
        P.op("dve", lambda E: E.reciprocal(out=den, in_=den), reads=[Bsmall], writes=[Bsmall])
        P.op("dve", lambda E: E.tensor_tensor(out=r, in0=ebc, in1=den, op=ALU.mult), reads=[Bsmall, Bgates], writes=[Bsmall])
        P.op("dve", lambda E: E.tensor_tensor(out=t, in0=ssq, in1=r, op=ALU.mult), reads=[Bsmall], writes=[Bsmall])
        P.op("dve", lambda E: E.tensor_tensor(out=t, in0=t, in1=r, op=ALU.mult), reads=[Bsmall], writes=[Bsmall])
        P.op("dve", lambda E: E.tensor_scalar(out=t, in0=t, scalar1=1.0 / DV, scalar2=EPS, op0=ALU.mult, op1=ALU.add), reads=[Bsmall], writes=[Bsmall])
        P.op("act", lambda E: E.activation(out=t, in_=t, func=AF.Sqrt), reads=[Bsmall], writes=[Bsmall])
        P.op("dve", lambda E: E.reciprocal(out=t, in_=t), reads=[Bsmall], writes=[Bsmall])
        P.op("dve", lambda E: E.tensor_tensor(out=t, in0=t, in1=r, op=ALU.mult), reads=[Bsmall], writes=[Bsmall])
        for h in range(H):
            P.op("dve", lambda E, h=h: E.scalar_tensor_tensor(out=ytok[:, h * 256:(h + 1) * 256], in0=numsb[:, h, 0:256], scalar=small[:, 32 + h:33 + h],
                                                             in1=GO[:, tt, h * 256:(h + 1) * 256], op0=ALU.mult, op1=ALU.mult),
                 reads=[Bnum, Bsmall, BGO[tt]], writes=[Bytok])
        for g4 in range(4):
            b = g4 % 2
            for j in range(4):
                fc = g4 * 4 + j
                P.op("pe", lambda E, b=b, j=j, fc=fc: E.transpose(out=psb[b][:, j * 128:(j + 1) * 128], in_=ytok[:, fc * 128:(fc + 1) * 128], identity=ident),
                     reads=[Bytok, Bconst], writes=[Bps[b]], signal=(j == 3))
            P.op("act", lambda E, b=b, g4=g4: E.copy(out=big[:, 16 + g4 * 4:16 + g4 * 4 + 4, tt * 128:(tt + 1) * 128],
                                                     in_=psb[b][:, 0:512].rearrange("p (j t) -> p j t", j=4)),
                 reads=[Bps[b]], writes=[Bbig[16 + g4 * 4 + j] for j in range(4)])

    for h in range(H):
        P.op("dve", lambda E, h=h: E.memset(C[:, h, :], 0.0), writes=[BC[h]])
    for tb in range(NBLK):
        load_x_and_norm1(tb)
        def gate_epi(fg, tt, b, w):
            P.op("dve", lambda E: E.tensor_tensor(out=gpre[:, tt, :], in0=ps[b][:, 0:16], in1=bif[:, :], op=ALU.add),
                 reads=[Bps[b], Bconst], writes=[Bgates])
        gemm_tok(w_in, D, O_G, 16, hnT_lhs, lambda tt: [BhnT[tt]], NTT, gate_epi)
        if tb == 0:
            dump("hnT_a", hnT[:, 0:16, :], [128, 16, TBX], BF16)
            dump("gpre", gpre[:], [128, NTT, 16])
        gates_block(tb)
        P.barrier()
        gemm_feat(w_in, D, O_K, 1024, hnT_rhs, BhnT, conv_silu_epi(kTb, Bk, 8), halo=True, pre=halo_pre)
        P.barrier()
        def v_epi(fg, tt, b, w):
            P.op("act", lambda E: E.copy(out=vb[:, tt, fg * 512:(fg + 1) * 512], in_=ps[b][:, :]), reads=[Bps[b]], writes=[Bv[tt]])
        gemm_tok(w_in, D, O_V, 2048, hnT_lhs, lambda tt: [BhnT[tt]], NTT, v_epi)
        if tb == 0:
            dump("gates_A", A_[:], [128, 8, 8])
            dump("gates_EB", EB[:], [128, 8, 8])
            dump("gates_EG", EG[:], [128, 8, 8])
            dump("kT", accb[:, 4096:8192], [128, H * TB], BF16)
            dump("v", accb[:, 8192:16384], [128, NTT * 2048], BF16)
        P.dma("sp", s_scr, lambda E, tb=tb: E.dma_start(out=kT_s[tb], in_=accb[:, 4096:8192]), reads=Bk, writes=[Bscr_k[tb]])
        P.dma("sp", s_scr, lambda E, tb=tb: E.dma_start(out=v_s[tb], in_=accb[:, 8192:16384]), reads=Bv, writes=[Bscr_v[tb]])
        for tt in range(NTT):
            state_update(tb * NTT + tt, tt, True)
    P.barrier()
    dump("C_local", C[:], [128, 8, 257])
    et = small[:, 40:48]
    P.op("dve", lambda E: E.tensor_copy(out=et, in_=EG[:, 0, :]), reads=[Bgates], writes=[Bsmall])
    for c in range(1, 8):
        P.op("dve", lambda E, c=c: E.tensor_tensor(out=et, in0=et, in1=EG[:, c, :], op=ALU.mult), reads=[Bsmall, Bgates], writes=[Bsmall])
    cc_in_ap = cc_in.ap()
    cc_out_ap = cc_out.ap()
    P.dma("sp", s_ld, lambda E: E.dma_start(out=cc_in_ap[:, 0:2056], in_=C[:].rearrange("p h e -> p (h e)")), reads=BC, writes=[Bcc])
    P.dma("sp", s_ld, lambda E: E.dma_start(out=cc_in_ap[:, 2056:CW], in_=et), reads=[Bsmall], writes=[Bcc])
    P.dma("pool", s_cc, lambda E: E.collective_compute("AllGather", ALU.bypass, replica_groups=[list(range(NCORES))],
                                                       ins=[cc_in.ap().opt()], outs=[cc_out.ap().opt()]),
          reads=[Bcc], writes=[Bcc], inc=1)
    P.op("dve", lambda E: E.memset(Rt[:, :, :], 0.0), writes=[BR])
    for h in range(H):
        P.op("dve", lambda E, h=h: E.memset(C[:, h, :], 0.0), reads=[], writes=[BC[h]])
    Cflat = C[:].rearrange("p h e -> p (h e)")
    Rflat = accr[:, 0:2056]
    for j in range(NCORES):
        i = j % 2
        if j < NCORES - 1:
            P.dma("sp", s_cj[i], lambda E, i=i, j=j: E.dma_start(out=Cj[i], in_=cc_out_ap[j * 128:(j + 1) * 128, :]), reads=[Bcc], writes=[BCj[i]])
        P.op("dve", lambda E, j=j: E.scalar_tensor_tensor(out=Cflat, in0=Rflat, scalar=onehot[:, j:j + 1], in1=Cflat, op0=ALU.mult, op1=ALU.add),
             reads=[BR, Bconst] + BC, writes=BC)
        if j < NCORES - 1:
            P.op("dve", lambda E, i=i: E.tensor_tensor(out=Rt[:, :, :], in0=Rt[:, :, :],
                                                      in1=Cj[i][:, 2056:CW].unsqueeze(2).to_broadcast([128, 8, 257]), op=ALU.mult),
                 reads=[BR, BCj[i]], writes=[BR])
            P.op("dve", lambda E, i=i: E.tensor_tensor(out=Rflat, in0=Rflat, in1=Cj[i][:, 0:2056], op=ALU.add), reads=[BR, BCj[i]], writes=[BR])
    for h in range(H):
        P.op("dve", lambda E, h=h: E.tensor_copy(out=Cbf[:, h, :], in_=C[:, h, :]), reads=[BC[h]], writes=[BCbf[h]])
    P.barrier()

    dump("C_in", C[:], [128, 8, 257])
    for tb in range(NBLK):
        load_x_and_norm1(tb)
        P.dma("sp", s_ld, lambda E, tb=tb: E.dma_start(out=accb[:, 4096:8192], in_=kT_s[tb]), reads=[Bscr_k[tb]], writes=Bk)
        P.dma("sp", s_ld, lambda E, tb=tb: E.dma_start(out=accb[:, 8192:16384], in_=v_s[tb]), reads=[Bscr_v[tb]], writes=Bv)
        gemm_feat(w_in, D, O_Q, 1024, hnT_rhs, BhnT, conv_silu_epi(qT, Bq, 0), halo=True, pre=halo_pre)
        P.barrier()
        def o_epi(fg, tt, b, w):
            i = tt % 2
            if tt == 0:
                pass
            P.op("act", lambda E: E.activation(out=gt1[i], in_=ps[b][:, :], func=AF.Sigmoid), reads=[Bps[b]], writes=[Bg1[i]])
            P.op("dve", lambda E: E.tensor_tensor(out=GO[:, tt, fg * 512:(fg + 1) * 512], in0=gt1[i], in1=bc512[fg % 2], op=ALU.mult),
                 reads=[Bg1[i], Bbc[fg % 2]], writes=[BGO[tt]])
        for fg in range(4):
            P.dma("sp", s_bc[fg % 2], lambda E, fg=fg: E.dma_start(out=bc512[fg % 2], in_=ghead_in[fg * 512:(fg + 1) * 512].partition_broadcast(128)),
                  writes=[Bbc[fg % 2]])
            gemm_tok(w_in, D, O_O + fg * 512, 512, hnT_lhs, lambda tt: [BhnT[tt]], NTT,
                     (lambda fg: (lambda f0, tt, b, w: o_epi(fg, tt, b, w)))(fg))
        P.barrier()
        for tt in range(NTT):
            c = tb * NTT + tt
            chunk_output(c, tt)
            state_update(c, tt, False)
        P.barrier()
        if tb == 0:
            dump("qT", accb[:, 0:4096], [128, H * TB], BF16)
            dump("GO", accb[:, 16384:24576], [128, NTT * 2048], BF16)
            dump("mix_mlstm", big[:, 16:32, :], [128, 16, TB], BF16)
        for g in range(4):
            wdw = 2 ** (g + 1)
            def pool_epi(fg, fc, b, g=g, wdw=wdw):
                e = fc % 2
                src = Et[fc]
                bufs = [(slev[e], Bsl[e]), (pwin[e], Brt[e])]
                cur, Bcur = src, BE[fc]
                k = 1
                lo = 0
                n = 0
                while k < wdw:
                    dst, Bd = bufs[n % 2]
                    lo2 = lo + k
                    P.op("dve", lambda E, dst=dst, cur=cur, lo2=lo2, k=k: E.tensor_tensor(out=dst[:, lo2:TBX], in0=cur[:, lo2:TBX], in1=cur[:, lo2 - k:TBX - k], op=ALU.add),
                         reads=[Bcur], writes=[Bd])
                    cur, Bcur = dst, Bd
                    lo = lo2
                    k *= 2
                    n += 1
                P.op("dve", lambda E, cur=cur: E.scalar_tensor_tensor(out=zT[:, fc, :], in0=cur[:, HALO:TBX], scalar=1.0 / wdw,
                                                                     in1=src[:, HALO:TBX], op0=ALU.mult, op1=ALU.subtract),
                     reads=[Bcur, BE[fc]], writes=[Bz[fc]])
                P.op("dve", lambda E, cur=cur, tb=tb: E.tensor_tensor(out=ptmp[:, e, :], in0=cur[:, HALO:HALO + 16], in1=icnt[:, tb, g, :], op=ALU.mult),
                     reads=[Bcur, Bconst], writes=[Bpt[e]])
                P.op("dve", lambda E: E.tensor_tensor(out=zT[:, fc, 0:16], in0=ptmp[:, e, :], in1=src[:, HALO:HALO + 16], op=ALU.subtract),
                     reads=[Bpt[e], BE[fc], Bz[fc]], writes=[Bz[fc]])
            zT = accb[:, 2 * 12288:2 * 12288 + 4 * TB].rearrange("p (c t) -> p c t", c=4)
            pwin = [accr[:, 14336 + i * 528:14336 + (i + 1) * 528] for i in range(2)]
            Bz = [Buf() for _ in range(4)]
            Bpt = [Buf(), Buf()]
            gemm_feat(w_in, D, O_POOL + g * 512, 512, hnT_rhs, BhnT, pool_epi, halo=True, pre=halo_pre)
            slot = load_block(w_pool, g * 512, 4, 0, 512)
            for fc in range(4):
                b = fc % 2
                for kc in range(4):
                    P.op("pe", lambda E, b=b, fc=fc, kc=kc, slot=slot: E.matmul(out=ps[b][:, :], lhsT=ring[slot][:, kc, fc * 128:(fc + 1) * 128],
                                                                               rhs=zT[:, kc, :], start=(kc == 0), stop=(kc == 3)),
                         reads=[Bring[slot], Bz[kc]], writes=[Bps[b]], signal=(kc == 3))
                ch = g * 4 + fc
                P.op("act", lambda E, b=b, ch=ch: E.activation(out=big[:, ch, :], in_=ps[b][:, :], func=AF.Copy, scale=psc[:, ch:ch + 1]),
                     reads=[Bps[b], Bconst], writes=[Bbig[ch]])
        P.barrier()
        if tb == 0:
            dump("mix_pool", big[:, 0:16, :], [128, 16, TB], BF16)
        r0 = tb * TB + HALO
        for tt in range(NTT):
            P.dma("sp", s_acc[tt], lambda E, tt=tt, r0=r0: E.dma_start(out=acc[tt], in_=x_ext[r0 + tt * 128:r0 + (tt + 1) * 128, :]), writes=[Bacc[tt]])

        def big_lhs(kk, tt):
            return big[:, kk, tt * 128:(tt + 1) * 128]

        def res_epi(fg, tt, b, w):
            P.op("dve", lambda E: E.tensor_tensor(out=acc[tt][:, fg * 512:(fg + 1) * 512], in0=ps[b][:, :], in1=acc[tt][:, fg * 512:(fg + 1) * 512], op=ALU.add),
                 reads=[Bps[b], Bacc[tt]], writes=[Bacc[tt]])
        gemm_tok(w_out, D, 0, D, big_lhs, lambda tt: Bbig, NTT, res_epi)
        if tb == 0:
            dump("h1_0", accr[:, 0:4096], [128, 4096])
            dump("h1_1", accr[:, 4096:8192], [128, 4096])
            dump("h1_2", accr[:, 8192:12288], [128, 4096])
            dump("h1_3", accr[:, 12288:16384], [128, 4096])
        for tt in range(NTT):
            norm_T(acc[tt], Bacc[tt], 128, 1, HALO + tt * 128, BhnT[tt])
        for hc in range(DFF // D):
            def up_epi(fg, fc, b):
                i = fc % 2
                P.op("dve", lambda E: E.tensor_scalar_max(out=rtmp[i], in0=ps[b][:, :], scalar1=0.0), reads=[Bps[b]], writes=[Brt[i]])
                P.op("act", lambda E: E.activation(out=big[:, fg * 4 + fc, :], in_=rtmp[i], func=AF.Square), reads=[Brt[i]], writes=[Bbig[fg * 4 + fc]])
            gemm_feat(w_up, D, hc * D, D, hnT_rhs, BhnT, up_epi, halo=False)
            gemm_tok(w_down[hc * D:(hc + 1) * D, :], D, 0, D, big_lhs, lambda tt: Bbig, NTT, res_epi)
        if tb == 0:
            dump("h2_0", accr[:, 0:4096], [128, 4096])
            dump("h2_1", accr[:, 4096:8192], [128, 4096])
            dump("h2_2", accr[:, 8192:12288], [128, 4096])
            dump("h2_3", accr[:, 12288:16384], [128, 4096])
        for tt in range(NTT):
            norm_T(acc[tt], Bacc[tt], 128, 2, HALO + tt * 128, BhnT[tt])
        for tt in range(NTT):
            rr = tb * TB + tt * 128
            P.dma("sp", s_p, lambda E, rr=rr: E.dma_start(out=pst, in_=p_in[rr:rr + 128, :]), writes=[Bp])
            P.op("dve", lambda E: E.tensor_copy(out=pbf, in_=pst), reads=[Bp], writes=[Bp])
            for j in range(2):
                P.op("pe", lambda E, j=j: E.transpose(out=psb[7][:, j * 128:(j + 1) * 128], in_=pbf[:, j * 128:(j + 1) * 128], identity=ident),
                     reads=[Bp, Bconst], writes=[Bps[7]], signal=(j == 1))
            P.op("act", lambda E, tt=tt: E.copy(out=pT[:, :, tt * 128:(tt + 1) * 128], in_=psb[7][:, 0:256].rearrange("p (j t) -> p j t", j=2)),
                 reads=[Bps[7]], writes=[BpT])
        P.barrier()
        for fg in range(D // 512):
            P.dma("sp", s_bc[fg % 2], lambda E, fg=fg: E.dma_start(out=bc512[fg % 2], in_=bgate_in[fg * 512:(fg + 1) * 512].partition_broadcast(128)),
                  writes=[Bbc[fg % 2]])
            st_gate = setctr["i"] % 2
            st_ple = 1 - st_gate

            def ple_epi(f0, tt, b, w, fg=fg, st_ple=st_ple):
                i = tt % 2
                bp = st_ple * 4 + tt
                P.op("dve", lambda E: E.tensor_tensor(out=gt1[i], in0=ps[b][:, :], in1=bc512[fg % 2], op=ALU.add), reads=[Bps[b], Bbc[fg % 2]], writes=[Bg1[i]])
                P.op("act", lambda E: E.activation(out=gt1[i], in_=gt1[i], func=AF.Sigmoid), reads=[Bg1[i]], writes=[Bg1[i]])
                P.op("dve", lambda E: E.tensor_tensor(out=gt2[i], in0=ps[bp][:, :], in1=gt1[i], op=ALU.mult), reads=[Bps[bp], Bg1[i]], writes=[Bg2[i]])
                P.op("dve", lambda E: E.tensor_tensor(out=acc[tt][:, fg * 512:(fg + 1) * 512], in0=gt2[i], in1=acc[tt][:, fg * 512:(fg + 1) * 512], op=ALU.add),
                     reads=[Bg2[i], Bacc[tt]], writes=[Bacc[tt]])
            slotp = load_block(w_ple, 0, 2, fg * 512, 512)
            for tt in range(NTT):
                bp = st_ple * 4 + tt
                for kc in range(2):
                    P.op("pe", lambda E, bp=bp, tt=tt, kc=kc, slotp=slotp: E.matmul(out=ps[bp][:, :], lhsT=pT[:, kc, tt * 128:(tt + 1) * 128],
                                                                                   rhs=ring[slotp][:, kc, :], start=(kc == 0), stop=(kc == 1)),
                         reads=[Bring[slotp], BpT], writes=[Bps[bp]], signal=(kc == 1))
            gemm_tok(w_gate, D, fg * 512, 512, hnT_lhs, lambda tt: [BhnT[tt]], NTT, ple_epi)
        P.barrier()
        if tb == 0:
            dump("h3_0", accr[:, 0:4096], [128, 4096])
            dump("h3_1", accr[:, 4096:8192], [128, 4096])
            dump("h3_2", accr[:, 8192:12288], [128, 4096])
            dump("h3_3", accr[:, 12288:16384], [128, 4096])
        P.dma("sp", s_ld, lambda E: E.dma_start(out=gfin, in_=gfin_in.partition_broadcast(128)), writes=[Bgfin] + BhnT)
        for tt in range(NTT):
            P.op("dve", lambda E: E.memset(small[:, 0:1], 0.0), writes=[Bsmall])
            P.op("act", lambda E, tt=tt: E.activation(out=hn[:, :], in_=acc[tt], func=AF.Square, accum_out=small[:, 0:1]),
                 reads=[Bacc[tt]], writes=[Bhn, Bsmall])
            rstd_from_ss(0, 128, 1.0 / D)
            P.op("dve", lambda E, tt=tt: E.scalar_tensor_tensor(out=acc[tt], in0=acc[tt], scalar=small[:, 0:1], in1=gfin, op0=ALU.mult, op1=ALU.mult),
                 reads=[Bacc[tt], Bsmall, Bgfin], writes=[Bacc[tt]])
            rr = tb * TB + tt * 128
            P.dma("sp", s_out, lambda E, tt=tt, rr=rr: E.dma_start(out=out[rr:rr + 128, :], in_=acc[tt]), reads=[Bacc[tt]])
        P.barrier()
    P.wait_event("sp", (s_out, P.cnt[s_out]))
    P.run()
    P.close()
    es.close()
    nc._dbg_names = dbg_names
    return nc


def _host_consts():
    s = np.arange(128)
    U = (s[:, None] <= s[None, :]).astype(np.float32)
    sel = np.zeros((128, 128), np.float32)
    sel[127, :] = 1.0
    cst = np.stack([U, sel, np.zeros((128, 128), np.float32)], axis=1)
    ident = np.eye(128, dtype=np.float32)
    cstb = np.stack([ident, U], axis=1).astype(ml_dtypes.bfloat16)
    return np.ascontiguousarray(cst), np.ascontiguousarray(cstb)


_NC_CACHE = {}


def kernel(x, p, g_mix, w_in, conv_w, conv_b, b_igate, b_fgate, g_head, w_pool, pool_scale,
           w_out, g_mlp, w_up, w_down, g_ple, w_ple_gate, b_ple_gate, w_ple, g_final):
    f32 = np.float32
    x = np.asarray(x, f32)[0]
    p = np.asarray(p, f32)[0, 0]
    cst, cstb = _host_consts()

    def fm(v, n):
        return np.ascontiguousarray(np.asarray(v, f32).reshape(n, 128).T)

    gT = np.concatenate([fm(g_mix[0], KC), fm(g_mlp[0], KC), fm(g_ple[0], KC)], axis=1)
    cwv = np.asarray(conv_w, f32)[0]
    cbv = np.asarray(conv_b, f32)[0]
    cw = np.stack([fm(cwv[0], 16), fm(cwv[1], 16), fm(cwv[2], 16), fm(cwv[3], 16), fm(cbv, 16)], axis=2)
    psc = fm(pool_scale[0], 16)
    bif = np.ascontiguousarray(np.tile(np.concatenate([np.asarray(b_igate, f32)[0], np.asarray(b_fgate, f32)[0]])[None, :], (128, 1)))
    common = {
        "w_in": np.asarray(w_in, f32)[0], "w_pool": np.asarray(w_pool, f32)[0].reshape(4 * 512, 512),
        "w_out": np.asarray(w_out, f32)[0], "w_up": np.asarray(w_up, f32)[0], "w_down": np.asarray(w_down, f32)[0],
        "w_gate": np.asarray(w_ple_gate, f32)[0], "w_ple": np.asarray(w_ple, f32)[0],
        "gT": gT, "cw": np.ascontiguousarray(cw), "psc": psc, "bif": bif,
        "g_head": np.asarray(g_head, f32)[0], "b_gate": np.asarray(b_ple_gate, f32)[0], "g_final": np.asarray(g_final, f32),
        "cst": cst, "cstb": cstb,
    }
    in_maps = []
    windows = (2, 4, 8, 16)
    for c in range(NCORES):
        xe = np.zeros((T + HALO, D), f32)
        xe[HALO:] = x[c * T:(c + 1) * T]
        if c > 0:
            xe[:HALO] = x[c * T - HALO:c * T]
        onehot = np.zeros((128, 8), f32)
        onehot[:, c] = 1.0
        icnt = np.zeros((128, NBLK, 4, 16), f32)
        for tb in range(NBLK):
            tg = c * T + tb * TB + np.arange(16)
            for gi, wdw in enumerate(windows):
                icnt[:, tb, gi, :] = (1.0 / np.minimum(tg + 1, wdw)).astype(f32)[None, :]
        m = dict(common)
        m.update({"x_ext": xe, "p": np.ascontiguousarray(p[c * T:(c + 1) * T]), "onehot": onehot, "icnt": icnt})
        in_maps.append(m)
    if "nc" not in _NC_CACHE:
        _NC_CACHE["nc"] = build_nc()
    res = run_bass_kernel_spmd(_NC_CACHE["nc"], in_maps, core_ids=list(range(NCORES)))
    if DEBUG:
        _NC_CACHE["dbg"] = [{n: np.asarray(r[n]) for n in _NC_CACHE["nc"]._dbg_names} for r in res.results]
    outp = np.concatenate([np.asarray(r["out"], f32) for r in res.results], axis=0)
    return outp.reshape(1, SEQ, D)
```
